# Optimizing a Trainium2 kernel written in Bass

```python
import math
import jax
import jax.numpy as jnp
from jax import lax
import numpy as np


D_MODEL = 2048
BATCH = 8
SEQ = 4096
DEPTH = 1

RWKV_WIDTH = D_MODEL // 2
RWKV_HEAD_DIM = 64
RWKV_HEADS = RWKV_WIDTH // RWKV_HEAD_DIM
DECAY_LORA = max(32, int(round(1.8 * RWKV_WIDTH ** 0.5 / 32)) * 32)
AAA_LORA = max(32, int(round(1.8 * RWKV_WIDTH ** 0.5 / 32)) * 32)
GATE_LORA = max(32, int(round(0.6 * RWKV_WIDTH ** 0.8 / 32)) * 32)
RWKV_COLS = 3 * RWKV_WIDTH + DECAY_LORA + AAA_LORA + GATE_LORA
RWKV_SPLITS = (RWKV_WIDTH, 2 * RWKV_WIDTH, 3 * RWKV_WIDTH,
               3 * RWKV_WIDTH + DECAY_LORA, 3 * RWKV_WIDTH + DECAY_LORA + AAA_LORA)
GN_EPS = 64e-5
L2_EPS = 1e-12

POOL_WIDTH = D_MODEL // 2
POOL_WINDOWS = (2, 4, 8, 16)
POOL_GROUPS = len(POOL_WINDOWS)
POOL_GROUP_DIM = POOL_WIDTH // POOL_GROUPS

IN_COLS = RWKV_COLS + POOL_WIDTH + 2 * D_MODEL
IN_SPLITS = (RWKV_COLS, RWKV_COLS + POOL_WIDTH, RWKV_COLS + POOL_WIDTH + D_MODEL)

D_FF = int(math.ceil(8 * D_MODEL / 3 / 256)) * 256
MACARON_WEIGHT = 0.5
NORM_EPS = 1e-6

kernel_name = 'rwkv7_pool_macaron_hybrid'


def _rmsnorm(x, g):
    xf = x.astype(jnp.float32)
    y = xf * lax.rsqrt(jnp.mean(xf * xf, axis=-1, keepdims=True) + NORM_EPS)
    return (y * g.astype(jnp.float32)).astype(x.dtype)


def _swiglu(x, w_gate, w_up, w_down):
    return (jax.nn.silu(x @ w_gate) * (x @ w_up)) @ w_down


def _token_shift(z):
    return jnp.pad(z, ((0, 0), (1, 0), (0, 0)))[:, :-1]


def _wkv7_scan(r, w, k, v, a, b):
    bsz, _, nh, nd = r.shape
    xs = tuple(jnp.moveaxis(t, 1, 0) for t in (r, w, k, v, a, b))

    def step(state, inp):
        r_t, w_t, k_t, v_t, a_t, b_t = inp
        sa = jnp.einsum('bhvk,bhk->bhv', state, a_t)
        state = (state * w_t[:, :, None, :]
                 + sa[..., None] * b_t[:, :, None, :]
                 + v_t[..., None] * k_t[:, :, None, :])
        y_t = jnp.einsum('bhvk,bhk->bhv', state, r_t)
        return state, y_t

    s0 = jnp.zeros((bsz, nh, nd, nd), jnp.float32)
    _, ys = lax.scan(step, s0, xs)
    return jnp.moveaxis(ys, 0, 1)


def _rwkv7_branch(z, mu, w0, w2, a0, a2, g2, k_k, k_a, r_k, gn_w, gn_b):
    bsz, seq, _ = z.shape
    f32 = jnp.float32
    zf = z.astype(f32)
    zs = zf + (_token_shift(zf) - zf) * mu.astype(f32)
    r, k, v, lw, la, lg = jnp.split(zs, RWKV_SPLITS, axis=-1)
    w = -jax.nn.softplus(-(w0.astype(f32) + jnp.tanh(lw) @ w2.astype(f32))) - 0.5
    decay = jnp.exp(-jnp.exp(w))
    a = jax.nn.sigmoid(a0.astype(f32) + la @ a2.astype(f32))
    g = jax.nn.sigmoid(lg) @ g2.astype(f32)
    hv = lambda t: t.reshape(bsz, seq, RWKV_HEADS, RWKV_HEAD_DIM)
    kk = hv(k * k_k.astype(f32))
    kk = kk / jnp.maximum(jnp.sqrt(jnp.sum(kk * kk, axis=-1, keepdims=True)), L2_EPS)
    a_h = hv(a)
    k = k * (1.0 + (a - 1.0) * k_a.astype(f32))
    r_h, k_h, v_h = hv(r), hv(k), hv(v)
    y = _wkv7_scan(r_h, hv(decay), k_h, v_h, -kk, kk * a_h)
    mean = jnp.mean(y, axis=-1, keepdims=True)
    var = jnp.mean(jnp.square(y - mean), axis=-1, keepdims=True)
    y = (y - mean) * lax.rsqrt(var + GN_EPS)
    y = y.reshape(bsz, seq, RWKV_WIDTH) * gn_w.astype(f32) + gn_b.astype(f32)
    bonus = jnp.sum(r_h * k_h * r_k.astype(f32), axis=-1, keepdims=True) * v_h
    y = y + bonus.reshape(bsz, seq, RWKV_WIDTH)
    return (y * g).astype(z.dtype)


def _pool_branch(z, pool_w, pool_scale):
    bsz, seq, _ = z.shape
    zf = z.astype(jnp.float32).reshape(bsz, seq, POOL_GROUPS, POOL_GROUP_DIM)
    c0 = jnp.pad(jnp.cumsum(zf, axis=1), ((0, 0), (1, 0), (0, 0), (0, 0)))
    t = jnp.arange(1, seq + 1, dtype=jnp.float32)
    outs = []
    for gi, win in enumerate(POOL_WINDOWS):
        cg = c0[:, :, gi]
        upper = cg[:, 1:]
        lower = jnp.pad(cg[:, :seq - win + 1], ((0, 0), (win - 1, 0), (0, 0)))
        cnt = jnp.minimum(t, float(win))[None, :, None]
        outs.append((upper - lower) / cnt)
    pooled = jnp.stack(outs, axis=2)
    mixed = pooled - zf
    y = jnp.einsum('bsgc,gcd->bsgd', mixed, pool_w.astype(jnp.float32))
    y = y.reshape(bsz, seq, POOL_WIDTH) * pool_scale.astype(jnp.float32)
    return y.astype(z.dtype)


def _hybrid_mixer(u, w_in, rwkv_mu, rwkv_w0, rwkv_w2, rwkv_a0, rwkv_a2, rwkv_g2, rwkv_k_k,
                  rwkv_k_a, rwkv_r_k, rwkv_gn_w, rwkv_gn_b, w_proj_a, pool_w, pool_scale,
                  w_proj_b, w_out):
    p = u @ w_in
    z_a, z_b, g_a, g_b = jnp.split(p, IN_SPLITS, axis=-1)
    y_a = _rwkv7_branch(z_a, rwkv_mu, rwkv_w0, rwkv_w2, rwkv_a0, rwkv_a2, rwkv_g2,
                        rwkv_k_k, rwkv_k_a, rwkv_r_k, rwkv_gn_w, rwkv_gn_b) @ w_proj_a
    y_b = _pool_branch(z_b, pool_w, pool_scale) @ w_proj_b
    m = jax.nn.sigmoid(g_a) * y_a + jax.nn.sigmoid(g_b) * y_b
    return m @ w_out


def setup_inputs(seed: int = 0) -> dict:
    key = jax.random.key(seed)
    ks = iter(jax.random.split(key, 40))
    L, D, W = DEPTH, D_MODEL, RWKV_WIDTH

    def nrm(shape, scale):
        return jax.random.normal(next(ks), shape, jnp.float32) * scale

    def gain(shape):
        return 1.0 + nrm(shape, 0.1)

    return {
        'x': nrm((BATCH, SEQ, D), 1.0),
        'ln_ffn1_pre': gain((L, D)),
        'ln_ffn1_post': gain((L, D)),
        'ffn1_gate': nrm((L, D, D_FF), D ** -0.5),
        'ffn1_up': nrm((L, D, D_FF), D ** -0.5),
        'ffn1_down': nrm((L, D_FF, D), D_FF ** -0.5),
        'ln_mix_pre': gain((L, D)),
        'ln_mix_post': gain((L, D)),
        'w_in': nrm((L, D, IN_COLS), D ** -0.5),
        'rwkv_mu': jax.random.uniform(next(ks), (L, RWKV_COLS), jnp.float32),
        'rwkv_w0': jax.random.uniform(next(ks), (L, W), jnp.float32, -6.0, -0.5),
        'rwkv_w2': nrm((L, DECAY_LORA, W), 0.1),
        'rwkv_a0': nrm((L, W), 0.3),
        'rwkv_a2': nrm((L, AAA_LORA, W), 0.1),
        'rwkv_g2': nrm((L, GATE_LORA, W), GATE_LORA ** -0.5),
        'rwkv_k_k': 0.85 + nrm((L, W), 0.1),
        'rwkv_k_a': gain((L, W)),
        'rwkv_r_k': nrm((L, RWKV_HEADS, RWKV_HEAD_DIM), 0.1),
        'rwkv_gn_w': gain((L, W)),
        'rwkv_gn_b': nrm((L, W), 0.01),
        'w_proj_a': nrm((L, W, D), W ** -0.5),
        'pool_w': nrm((L, POOL_GROUPS, POOL_GROUP_DIM, POOL_GROUP_DIM), POOL_GROUP_DIM ** -0.5),
        'pool_scale': gain((L, POOL_WIDTH)),
        'w_proj_b': nrm((L, POOL_WIDTH, D), POOL_WIDTH ** -0.5),
        'w_out': nrm((L, D, D), D ** -0.5),
        'ln_ffn2_pre': gain((L, D)),
        'ln_ffn2_post': gain((L, D)),
        'ffn2_gate': nrm((L, D, D_FF), D ** -0.5),
        'ffn2_up': nrm((L, D, D_FF), D ** -0.5),
        'ffn2_down': nrm((L, D_FF, D), D_FF ** -0.5),
    }


def reference(x, ln_ffn1_pre, ln_ffn1_post, ffn1_gate, ffn1_up, ffn1_down, ln_mix_pre,
              ln_mix_post, w_in, rwkv_mu, rwkv_w0, rwkv_w2, rwkv_a0, rwkv_a2, rwkv_g2,
              rwkv_k_k, rwkv_k_a, rwkv_r_k, rwkv_gn_w, rwkv_gn_b, w_proj_a, pool_w, pool_scale,
              w_proj_b, w_out, ln_ffn2_pre, ln_ffn2_post, ffn2_gate, ffn2_up, ffn2_down):
    h = x
    for l in range(DEPTH):
        f = _swiglu(_rmsnorm(h, ln_ffn1_pre[l]), ffn1_gate[l], ffn1_up[l], ffn1_down[l])
        h = h + MACARON_WEIGHT * _rmsnorm(f, ln_ffn1_post[l])
        mx = _hybrid_mixer(_rmsnorm(h, ln_mix_pre[l]), w_in[l], rwkv_mu[l], rwkv_w0[l],
                           rwkv_w2[l], rwkv_a0[l], rwkv_a2[l], rwkv_g2[l], rwkv_k_k[l],
                           rwkv_k_a[l], rwkv_r_k[l], rwkv_gn_w[l], rwkv_gn_b[l], w_proj_a[l],
                           pool_w[l], pool_scale[l], w_proj_b[l], w_out[l])
        h = h + _rmsnorm(mx, ln_mix_post[l])
        f = _swiglu(_rmsnorm(h, ln_ffn2_pre[l]), ffn2_gate[l], ffn2_up[l], ffn2_down[l])
        h = h + MACARON_WEIGHT * _rmsnorm(f, ln_ffn2_post[l])
    return h
```

```python
import os
import numpy as np
from contextlib import ExitStack
import concourse.bass as bass
import concourse.mybir as mybir
from concourse.bass_utils import run_bass_kernel_spmd

F32 = mybir.dt.float32
BF16 = mybir.dt.bfloat16
AF = mybir.ActivationFunctionType
ALU = mybir.AluOpType

D = 2048
DFF = 5632
T = 512
KC = D // 128
FC = DFF // 128
INC = 8480
SEQ = 4096
NCORES = 8
SBUF_BASE = 16640
NSLAB = 304
SBUF_BYTES = 229376 - 64

V_L1PRE, V_L1POST, V_LMPRE, V_LMPOST, V_L2PRE, V_L2POST = 0, 16, 32, 48, 64, 80
V_MU, V_W0, V_A0, V_KK, V_KA, V_RK, V_GNW, V_GNB, V_PS = 96, 123, 131, 139, 147, 155, 163, 171, 179
NV = 187


class Op:
    __slots__ = ("eng", "fn", "chan", "deps", "sig", "cnt", "waits", "idx")

    def __init__(self, eng, fn, chan):
        self.eng, self.fn, self.chan = eng, fn, chan
        self.deps = set()
        self.sig = False
        self.cnt = 0
        self.waits = []
        self.idx = -1


def _esize(dt):
    return 2 if dt == BF16 else 4


def ap_keys(ap):
    t = ap.tensor
    tn = type(t).__name__
    pairs = ap.ap
    pst = pairs[0][0]
    npart = pairs[0][1]
    off = ap.offset
    if pst > 0:
        p0 = off // pst
        fo = off % pst
    else:
        p0, fo = 0, off
    halves = []
    if p0 < 64:
        halves.append(0)
    if p0 + npart > 64:
        halves.append(1)
    if tn.startswith("PSum"):
        return [("P", t.name, h) for h in halves]
    lo0 = t.manual_sbuf_range[0]
    es = _esize(ap.dtype)
    ext = 1
    for s, c in pairs[1:]:
        ext += (c - 1) * abs(s)
    lo = lo0 + fo * es
    hi = lo0 + (fo + ext) * es
    ks = []
    for g in range(lo // 256, (hi + 255) // 256):
        for h in halves:
            ks.append((g, h))
    return ks


class Sched:
    ENGS = ("pe", "act", "dve", "pool", "sp")

    def __init__(self):
        self.ops = []
        self.last_w = {}
        self.readers = {}

    def src_of(self, op):
        return op.chan if op.chan is not None else op.eng

    def op(self, eng, fn, r=(), w=(), rk=(), wk=(), chan=None):
        o = Op(eng, fn, chan)
        o.idx = len(self.ops)
        src = self.src_of(o)
        same = (chan is None)
        rkeys = list(rk)
        for a in r:
            rkeys.extend(ap_keys(a))
        wkeys = list(wk)
        for a in w:
            wkeys.extend(ap_keys(a))
        deps = o.deps
        ops = self.ops
        lw = self.last_w
        rd = self.readers
        for k in rkeys:
            x = lw.get(k)
            if x is not None:
                deps.add(x)
        for k in wkeys:
            x = lw.get(k)
            if x is not None:
                a = ops[x]
                if not (same and a.chan is None and a.eng == eng):
                    deps.add(x)
            rs = rd.get(k)
            if rs:
                for s, x in rs.items():
                    if same and s == eng:
                        continue
                    deps.add(x)
        for k in rkeys:
            d = rd.get(k)
            if d is None:
                rd[k] = {src: o.idx}
            else:
                d[src] = o.idx
        for k in wkeys:
            lw[k] = o.idx
            rd[k] = {}
        deps.discard(o.idx)
        ops.append(o)
        return o

    def finalize(self):
        ops = self.ops

        def skip(a, b):
            return a.eng == "pe" and b.eng == "pe" and a.chan is None and b.chan is None

        ordn = {}
        for o in ops:
            s = self.src_of(o)
            ordn[s] = ordn.get(s, 0) + 1
            o.cnt = ordn[s]
        known = {e: {} for e in self.ENGS}
        frozen = {e: {} for e in self.ENGS}
        dirty = {e: False for e in self.ENGS}
        snap = [None] * len(ops)
        needed = set()
        for o in ops:
            kn = known[o.eng]
            need = {}
            for d in o.deps:
                a = ops[d]
                if skip(a, o):
                    continue
                s = self.src_of(a)
                if kn.get(s, 0) >= a.cnt:
                    continue
                if need.get(s, (0, None))[0] < a.cnt:
                    need[s] = (a.cnt, a)
            for s, (c, a) in need.items():
                if kn.get(s, 0) >= c:
                    continue
                o.waits.append(a.idx)
                needed.add(a.idx)
                kn[s] = c
                dirty[o.eng] = True
                base, s_own, c_own = snap[a.idx]
                for s2, c2 in base.items():
                    if kn.get(s2, 0) < c2:
                        kn[s2] = c2
                if s_own is not None and kn.get(s_own, 0) < c_own:
                    kn[s_own] = c_own
            if dirty[o.eng]:
                frozen[o.eng] = dict(kn)
                dirty[o.eng] = False
            snap[o.idx] = (frozen[o.eng], (o.eng if o.chan is None else None), o.cnt)
            o.deps = None
        counts = {}
        for o in ops:
            o.sig = (o.chan is not None) or (o.idx in needed)
            if o.sig:
                s = self.src_of(o)
                counts[s] = counts.get(s, 0) + 1
                o.cnt = counts[s]
        for o in ops:
            o.waits = [(self.src_of(ops[i]), ops[i].cnt) for i in o.waits]
        self.sources = list(counts.keys())
        return counts

    def emit(self, block, sems, final_waits=()):
        per = {e: [] for e in self.ENGS}
        last_cnt = {}
        for o in self.ops:
            per[o.eng].append(o)
            if o.sig:
                last_cnt[self.src_of(o)] = o.cnt
        engs = self.ENGS

        def unit(s):
            return 1 if s in engs else 16

        def run(name, e):
            for o in per[name]:
                for (s, c) in o.waits:
                    e.wait_ge(sems[s], c * unit(s))
                ins = o.fn(e)
                if o.sig:
                    s = self.src_of(o)
                    ins.then_inc(sems[s], unit(s))
            for (en, s) in final_waits:
                if en == name and s in last_cnt:
                    e.wait_ge(sems[s], last_cnt[s] * unit(s))

        block.tensor(lambda e: run("pe", e))
        block.scalar(lambda e: run("act", e))
        block.vector(lambda e: run("dve", e))
        block.gpsimd(lambda e: run("pool", e))
        block.sync(lambda e: run("sp", e))


WSPEC = [("g1", D, DFF), ("u1", D, DFF), ("d1", DFF, D), ("win", D, INC), ("pb", 1024, D),
         ("pa", 1024, D), ("wo", D, D), ("g2", D, DFF), ("u2", D, DFF), ("d2", DFF, D)]


KROWS = {n: K for n, K, M in WSPEC}


def build(NT, stage=3, conv=True, skip=()):
    nc = bass.Bass("TRN2", target_bir_lowering=False)
    S = Sched()
    NTOK = NT * T
    x = nc.dram_tensor("x", [NTOK, D], F32, kind="ExternalInput").ap()
    out = nc.dram_tensor("out", [NTOK, D], F32, kind="ExternalOutput").ap()
    wf, wb = {}, {}
    for n, K, M in WSPEC:
        wf[n] = nc.dram_tensor(n, [K, M], F32, kind="ExternalInput").ap()
    wscs = [nc.dram_tensor("wsc%d" % i, [76, 128, 4096], BF16, kind="Internal").ap() for i in range(4)]
    vec_d = nc.dram_tensor("vec", [128, NV], F32, kind="ExternalInput").ap()
    wa2_d = nc.dram_tensor("wa2", [128, 1024], F32, kind="ExternalInput").ap()
    g2a_d = nc.dram_tensor("g2a", [128, 1024], F32, kind="ExternalInput").ap()
    g2b_d = nc.dram_tensor("g2b", [32, 1024], F32, kind="ExternalInput").ap()
    pw_d = nc.dram_tensor("poolw", [128, 8, 256], F32, kind="ExternalInput").ap()

    cur = [SBUF_BASE]
    cnt = [0]

    def alloc(shape, dt, at=None):
        n = 1
        for s in shape[1:]:
            n *= s
        nb = n * _esize(dt)
        nb = (nb + 63) // 64 * 64
        if at is None:
            at = cur[0]
            cur[0] += nb
            assert cur[0] <= SBUF_BYTES, ("SBUF overflow", cur[0])
        cnt[0] += 1
        return nc.alloc_sbuf_tensor_at("t%d" % cnt[0], list(shape), dt, offset=at)

    vec = alloc([128, NV], F32)
    cst = alloc([128, 8], F32)
    identf = alloc([128, 128], F32)
    identb = alloc([128, 128], BF16)
    onesb = alloc([128, 128], BF16)
    blkb = alloc([128, 128], BF16)
    mask12 = alloc([128, 8, 128], BF16)
    mask3 = alloc([128, 8, 64], BF16)
    irep = alloc([128, 8, 64], BF16)
    rmask = alloc([128, T], BF16)
    wa2 = alloc([128, 1024], BF16)
    g2a = alloc([128, 1024], BF16)
    g2b = alloc([128, 1024], BF16)
    poolw = alloc([128, 8, 256], BF16)
    zc = alloc([128, 32], F32)
    hist = alloc([128, 8, 16], F32)
    H0 = alloc([128, 8, 64], F32)
    WCt = alloc([128, 8, 8], F32)
    rs = alloc([128, T], F32)
    sqs = [alloc([128, T], BF16) for _ in range(2)]
    h = alloc([128, KC, T], F32)
    xn = alloc([128, KC, T], BF16)
    stg = [alloc([128, 1024], F32) for _ in range(2)]
    NSLOT = 3
    wslots = [alloc([128, 4096], BF16) for _ in range(NSLOT)]
    base = cur[0]
    act_t = alloc([128, FC, T], BF16)
    f_t = alloc([128, KC, T], F32)
    ffn_end = cur[0]
    sgs = [alloc([128, T], F32) for _ in range(2)]
    cur[0] = base
    ARt = alloc([128, 8, 8, 128], BF16)
    Bt = alloc([128, 8, T], BF16)
    Kt = alloc([128, 8, T], BF16)
    Vt = alloc([128, 8, T], BF16)
    BON = alloc([128, 8, T], BF16)
    GG = alloc([128, 8, T], BF16)
    Y = alloc([128, 8, T], F32)
    mix_end = cur[0]
    MIX = alloc([128, 8, T], BF16, at=Kt.manual_sbuf_range[0])
    YB = alloc([128, 8, T], BF16, at=Vt.manual_sbuf_range[0])
    Mg = alloc([128, KC, T], BF16, at=ARt.manual_sbuf_range[0])
    YA = alloc([128, 8, T], BF16, at=Bt.manual_sbuf_range[0])
    MX = alloc([128, KC, T], F32, at=Kt.manual_sbuf_range[0])
    assert Kt.manual_sbuf_range[0] + 32768 <= Y.manual_sbuf_range[0]
    tmp0 = cur[0]
    nm = {k: alloc([128, 528 if k in ("R", "K", "V") else T], F32)
          for k in ("R", "K", "V", "A", "LW", "KKN", "KN", "CUM")}
    nm["E"] = nm["CUM"]
    cumc = alloc([128, 8], F32)
    scr = [alloc([128, 528], F32) for _ in range(4)]
    LAT = alloc([128, T], BF16)
    SGA = alloc([128, T], BF16)
    SGB = alloc([128, T], BF16)
    scb = sqs
    prep_end = cur[0]
    cur[0] = tmp0
    SC1 = alloc([128, 8, 128], BF16)
    SC2 = alloc([128, 8, 128], BF16)
    Pb = [alloc([128, 8, 64], BF16) for _ in range(2)]
    Qb = [alloc([128, 8, 64], BF16) for _ in range(2)]
    Gb = [alloc([128, 8, 64], BF16) for _ in range(2)]
    VT = alloc([128, 8, 64], BF16)
    BT = alloc([128, 8, 64], BF16)
    KT = alloc([128, 8, 64], BF16)
    X1b = alloc([128, 8, 64], BF16)
    Ub = alloc([128, 8, 64], BF16)
    H0p = alloc([128, 8, 64], F32)
    H0pb = alloc([128, 8, 64], BF16)
    cur[0] = max(prep_end, cur[0], ffn_end + 2 * 2048)
    assert cur[0] <= SBUF_BYTES, cur[0]

    psum = [nc.alloc_psum_tensor("ps%d" % i, [128, 512], F32) for i in range(8)]
    pctr = [0]

    def PS():
        p = psum[pctr[0] % 8]
        pctr[0] += 1
        return p

    rot = {"scr": 0, "scb": 0, "sq": 0, "sg": 0, "ws": 0, "stg": 0}

    def nxt(name, lst):
        v = lst[rot[name] % len(lst)]
        rot[name] += 1
        return v

    def mm(o, lhsT, rhs, start=True, stop=True):
        S.op("pe", lambda e: e.matmul(o, lhsT=lhsT, rhs=rhs, start=start, stop=stop), r=[lhsT, rhs], w=[o])

    def tr(o, in_, ident):
        S.op("pe", lambda e: e.transpose(o, in_, ident), r=[in_, ident], w=[o])

    def act(o, in_, func, bias=None, scale=None, eng="act"):
        r = [in_]
        kw = {}
        if bias is not None:
            kw["bias"] = bias
            r.append(bias)
        if scale is not None:
            kw["scale"] = scale
            if not isinstance(scale, float):
                r.append(scale)
        S.op("act", lambda e: e.activation(out=o, in_=in_, func=func, **kw), r=r, w=[o])

    def cp(eng, o, in_):
        if eng == "act":
            S.op("act", lambda e: e.copy(out=o, in_=in_), r=[in_], w=[o])
        else:
            S.op(eng, lambda e: e.tensor_copy(out=o, in_=in_), r=[in_], w=[o])

    def tt(eng, o, a, b, op):
        S.op(eng, lambda e: e.tensor_tensor(out=o, in0=a, in1=b, op=op), r=[a, b], w=[o])

    def ts(eng, o, a, s1, s2, op0, op1=None):
        r = [a] + [s for s in (s1, s2) if s is not None and not isinstance(s, float)]
        if op1 is None:
            S.op(eng, lambda e: e.tensor_scalar(out=o, in0=a, scalar1=s1, scalar2=None, op0=op0), r=r, w=[o])
        else:
            S.op(eng, lambda e: e.tensor_scalar(out=o, in0=a, scalar1=s1, scalar2=s2, op0=op0, op1=op1), r=r, w=[o])

    def stt(o, in0, sc, in1, op0, op1):
        r = [in0, in1] + ([] if isinstance(sc, float) else [sc])
        S.op("dve", lambda e: e.scalar_tensor_tensor(out=o, in0=in0, scalar=sc, in1=in1, op0=op0, op1=op1), r=r, w=[o])

    def recip(o, in_):
        S.op("dve", lambda e: e.reciprocal(out=o, in_=in_), r=[in_], w=[o])

    def memset(eng, o, v):
        S.op(eng, lambda e: e.memset(o, v), w=[o])

    def vcol(c):
        return vec[:, c:c + 1]

    S.op("sp", lambda e: e.dma_start(out=vec[:], in_=vec_d), w=[vec[:]], chan="c0")
    memset("pool", cst[:, 0:1], 1e-6)
    memset("pool", cst[:, 1:2], 64e-5)
    memset("pool", cst[:, 2:3], 0.0)
    memset("pool", identf[:], 1.0)
    S.op("pool", lambda e: e.affine_select(out=identf[:], in_=identf[:], pattern=[[-1, 128]], compare_op=ALU.is_equal,
                                           fill=0.0, base=0, channel_multiplier=1), r=[identf[:]], w=[identf[:]])
    memset("pool", onesb[:], 1.0)
    memset("pool", blkb[:], 0.0)
    memset("pool", blkb[0:64, 0:64], 1.0)
    memset("pool", blkb[64:128, 64:128], 1.0)
    memset("pool", rmask[:], 1.0)
    memset("pool", rmask[:].rearrange("p (c t) -> p c t", t=64)[:, :, 0:1], 0.0)
    memset("pool", zc[:], 0.0)
    memset("pool", hist[:], 0.0)
    memset("pool", H0[:], 0.0)
    mtmp = [nc.alloc_sbuf_tensor_at("mtmp%d" % i, [128, 8, 64], F32, offset=base + i * 2048) for i in range(9)]
    onesf = mtmp[8][:]
    memset("pool", onesf, 1.0)
    mspecs = [(mask12[:, :, 0:64], ALU.is_gt, -1, 1),
              (mask12[:, :, 64:128], ALU.is_ge, -1, 1),
              (mask3[:], ALU.is_gt, 1, -1),
              (irep[:], ALU.is_equal, 1, -1)]

    def asel(o, cmp_, cm, st, base_, last=False):
        S.op("pool", lambda e: e.affine_select(out=o, in_=onesf, pattern=[[0, 8], [st, 64]], compare_op=cmp_,
                                               fill=0.0, base=base_, channel_multiplier=cm), r=[onesf], w=[o],
             wk=(["POOLDONE"] if last else []))

    for i, (dst, cmp_, cm, st) in enumerate(mspecs):
        asel(mtmp[2 * i][:], cmp_, cm, st, 0)
        asel(mtmp[2 * i + 1][:], cmp_, cm, st, -64 * cm, last=(i == 3))
    S.op("dve", lambda e: e.tensor_copy(out=identb[:], in_=identf[:]), r=[identf[:]], w=[identb[:]], rk=["POOLDONE"])
    for i, (dst, cmp_, cm, st) in enumerate(mspecs):
        cp("dve", dst[0:64], mtmp[2 * i][0:64])
        cp("dve", dst[64:128], mtmp[2 * i + 1][64:128])

    first_tile = [True]
    castrr = [0]

    def stage_cast(dst_view, src_ap, np_=128):
        si = rot["stg"] % 2
        rot["stg"] += 1
        st_ = stg[si]
        shp = list(dst_view.shape)
        nel = 1
        for v_ in shp[1:]:
            nel *= v_
        sv = st_[0:np_, 0:nel]
        if len(shp) == 3:
            sv = sv.rearrange("p (c m) -> p c m", m=shp[2])
        S.op("sp", lambda e: e.dma_start(out=sv, in_=src_ap), w=[sv], chan="sg%d" % si)
        castrr[0] += 1
        cp("act" if castrr[0] % 2 else "dve", dst_view, sv)

    slab_ids = {}

    def load_slab(n, k0, nkc, m0, mw):
        slot = nxt("ws", list(range(NSLOT)))
        nel = nkc * mw
        flat = wslots[slot][:, 0:nel]
        view = flat.rearrange("p (c m) -> p c m", m=mw)
        key = ("wb", n, k0, m0)
        if key not in slab_ids:
            slab_ids[key] = len(slab_ids)
            assert len(slab_ids) <= NSLAB
        sid = slab_ids[key]
        dram_b = wscs[sid // 76][sid % 76, :, 0:nel]
        if first_tile[0]:
            npc = 1024 // mw
            for c0 in range(0, nkc, npc):
                n_ = min(npc, nkc - c0)
                src = wf[n][(k0 + c0) * 128:(k0 + c0 + n_) * 128, m0:m0 + mw].rearrange("(c p) m -> p c m", p=128)
                stage_cast(view[:, c0:c0 + n_, :], src)
            S.op("sp", lambda e: e.dma_start(out=dram_b, in_=flat), r=[flat], wk=[key], chan="wst%d" % slot)
        else:
            S.op("sp", lambda e: e.dma_start(out=flat, in_=dram_b), rk=[key], w=[flat], chan="ws%d" % slot)
        return view

    stage_cast(wa2[:], wa2_d)
    stage_cast(g2a[:], g2a_d)
    stage_cast(g2b[0:32, :], g2b_d, np_=32)
    stage_cast(poolw[:, 0:4, :], pw_d[:, 0:4, :])
    stage_cast(poolw[:, 4:8, :], pw_d[:, 4:8, :])

    def linear(n, rhs_chunks, m0, mtot, consume, mw=256):
        nk = len(rhs_chunks)
        parts = [(k0, min(16, nk - k0)) for k0 in range(0, nk, 16)]
        for s0 in range(0, mtot, mw):
            w_ = min(mw, mtot - s0)
            nj = (w_ + 127) // 128
            pss = [PS() for _ in range(nj)]
            for (k0, nkc) in parts:
                slab = load_slab(n, k0, nkc, m0 + s0, w_)
                for j in range(nj):
                    cw = min(128, w_ - j * 128)
                    for kc in range(nkc):
                        mm(pss[j][0:cw, :], slab[:, kc, j * 128:j * 128 + cw], rhs_chunks[k0 + kc],
                           start=(k0 + kc == 0), stop=(k0 + kc == nk - 1))
            for j in range(nj):
                consume(s0 // 128 + j, pss[j])

    def stats(chunks, n_feat):
        p = PS()
        for c, ch in enumerate(chunks):
            sq = nxt("sq", sqs)
            act(sq[:], ch, AF.Square)
            mm(p[:], onesb[:], sq[:], start=(c == 0), stop=(c == len(chunks) - 1))
        act(rs[:], p[:], AF.Sqrt, bias=cst[:, 0:1], scale=1.0 / n_feat)
        recip(rs[:], rs[:])

    def prenorm(gcol):
        stats([h[:, c, :] for c in range(KC)], D)
        for c in range(KC):
            stt(xn[:, c, :], h[:, c, :], vcol(gcol + c), rs[:], ALU.mult, ALU.mult)

    def postnorm_residual(src, gcol, wgt):
        stats([src[:, c, :] for c in range(KC)], D)
        for c in range(KC):
            t1 = nxt("scr", scr)[:, 0:T]
            stt(t1, src[:, c, :], vcol(gcol + c), rs[:], ALU.mult, ALU.mult)
            stt(h[:, c, :], t1, wgt, h[:, c, :], ALU.mult, ALU.add)

    def ffn(gn, un, dn, pre, post):
        prenorm(pre)
        xch = [xn[:, c, :] for c in range(KC)]
        pend = {}

        def cons_g(j, p):
            pend[j] = p

        for jb in range(FC // 2):
            def cons_u(j, p, jb=jb):
                sg = nxt("sg", sgs)
                act(sg[:], pend.pop(j)[:], AF.Silu)
                tt("dve", act_t[:, jb * 2 + j, :], sg[:], p[:], ALU.mult)

            linear(gn, xch, jb * 256, 256, cons_g)
            linear(un, xch, jb * 256, 256, cons_u)

        def cons_d(j, p):
            cp("act", f_t[:, j, :], p[:])

        linear(dn, [act_t[:, c, :] for c in range(FC)], 0, D, cons_d)
        postnorm_residual(f_t, post, 0.5)

    def lerp(p, np_, zidx, o):
        z1 = nxt("scr", scr)
        cp("dve", z1[0:np_, 0:1], zc[0:np_, zidx:zidx + 1])
        cp("act", z1[0:np_, 1:T + 1], p[0:np_, :])
        cp("dve", zc[0:np_, zidx:zidx + 1], z1[0:np_, T:T + 1])
        d = nxt("scr", scr)
        tt("dve", d[0:np_, 0:T], z1[0:np_, 0:T], p[0:np_, :], ALU.subtract)
        stt(o, d[0:np_, 0:T], vec[0:np_, V_MU + zidx:V_MU + zidx + 1], z1[0:np_, 1:T + 1], ALU.mult, ALU.add)

    def c3(ap_):
        return ap_.rearrange("p (c t) -> p c t", t=64)

    def mixer(ti):
        MST = int(os.environ.get('MST', '9'))
        prenorm(V_LMPRE)
        un_ = [xn[:, c, :] for c in range(KC)]
        zl = nm["E"]

        def cons_lat(j, p):
            lerp(p, 128, 24, zl[:])
            act(LAT[0:64, :], zl[0:64, :], AF.Tanh)
            cp("act", LAT[64:128, :], zl[64:128, :])

        linear("win", un_, 3072, 128, cons_lat)

        def cons_gl(j, p):
            if j == 0:
                lerp(p, 128, 25, zl[:])
                act(SGA[:], zl[:], AF.Sigmoid)
            else:
                lerp(p, 32, 26, zl[0:32, :])
                act(SGB[0:32, :], zl[0:32, :], AF.Sigmoid)

        linear("win", un_, 3200, 160, cons_gl)

        if MST < 3:
            return
        R, K_, V_, A_, LW, KKN, KN, CUM, E_ = (nm[k] for k in ("R", "K", "V", "A", "LW", "KKN", "KN", "CUM", "E"))
        R, K_, V_ = R[:, 0:T], K_[:, 0:T], V_[:, 0:T]
        for fc in range(8):
            got = {}

            def mk(name):
                def c_(j, p):
                    got[(name, j)] = p
                return c_

            linear("win", un_, fc * 128, 128, mk("r"), mw=128)
            linear("win", un_, 1024 + fc * 128, 128, mk("k"), mw=128)
            linear("win", un_, 2048 + fc * 128, 128, mk("v"), mw=128)
            for j in range(1):
                cols = slice(fc * 128, (fc + 1) * 128)
                lerp(got[("r", j)], 128, fc, R)
                lerp(got[("k", j)], 128, 8 + fc, K_)
                lerp(got[("v", j)], 128, 16 + fc, V_)
                pw = PS()
                mm(pw[:], wa2[0:64, cols], LAT[0:64, :])
                pa_ = PS()
                mm(pa_[:], wa2[64:128, cols], LAT[64:128, :])
                pg = PS()
                mm(pg[:], g2a[:, cols], SGA[:], start=True, stop=False)
                mm(pg[:], g2b[0:32, cols], SGB[0:32, :], start=False, stop=True)
                act(LW[:], pw[:], AF.Sigmoid, bias=vcol(V_W0 + fc))
                act(A_[:], pa_[:], AF.Sigmoid, bias=vcol(V_A0 + fc))
                cp("act", GG[:, fc, :], pg[:])
                ts("dve", LW[:], LW[:], -0.6065306597126334, None, ALU.mult)
                ts("dve", KKN[:], K_, vcol(V_KK + fc), None, ALU.mult)
                sqb = nxt("scb", scb)
                act(sqb[:], KKN[:], AF.Square)
                pn = PS()
                mm(pn[:], blkb[:], sqb[:])
                t2 = nxt("scr", scr)[:, 0:T]
                act(t2, pn[:], AF.Sqrt, bias=cst[:, 2:3], scale=1.0)
                ts("dve", t2, t2, 1e-12, None, ALU.max)
                recip(t2, t2)
                tt("dve", KKN[:], KKN[:], t2, ALU.mult)
                t3 = nxt("scr", scr)[:, 0:T]
                ts("dve", t3, A_[:], 1.0, vcol(V_KA + fc), ALU.subtract, ALU.mult)
                stt(KN[:], t3, 1.0, K_, ALU.add, ALU.mult)
                rkb = nxt("scb", scb)
                stt(rkb[:], R, vcol(V_RK + fc), KN[:], ALU.mult, ALU.mult)
                pb_ = PS()
                mm(pb_[:], blkb[:], rkb[:])
                tt("dve", BON[:, fc, :], pb_[:], V_, ALU.mult)
                S.op("dve", lambda e: e.tensor_tensor_scan(out=CUM[:], data0=rmask[:], data1=LW[:], initial=0.0,
                                                           op0=ALU.mult, op1=ALU.add), r=[rmask[:], LW[:]], w=[CUM[:]])
                cp("dve", cumc[:], c3(CUM[:])[:, :, 63])
                act(WCt[:, fc, :], cumc[:], AF.Exp)
                tt("dve", c3(E_[:]), cumc[:].unsqueeze(2).to_broadcast([128, 8, 64]), c3(CUM[:]), ALU.subtract)
                t5 = nxt("scr", scr)[:, 0:T]
                act(t5, E_[:], AF.Exp)
                tt("dve", Kt[:, fc, :], KN[:], t5, ALU.mult)
                t6 = nxt("scr", scr)[:, 0:T]
                tt("dve", t6, KKN[:], A_[:], ALU.mult)
                tt("dve", Bt[:, fc, :], t6, t5, ALU.mult)
                t7 = nxt("scr", scr)[:, 0:T]
                act(t7, E_[:], AF.Exp, scale=-1.0)
                tt("dve", ARt[:, fc, :, 64:128], c3(R), c3(t7), ALU.mult)
                t8 = nxt("scr", scr)[:, 0:T]
                tt("dve", t8, E_[:], LW[:], ALU.add)
                act(t8, t8, AF.Exp, scale=-1.0)
                stt(ARt[:, fc, :, 0:64], c3(KKN[:]), -1.0, c3(t8), ALU.mult, ALU.mult)
                cp("act", Vt[:, fc, :], V_)

        if MST < 4:
            return
        for c in range(T // 64):
            cs = slice(c * 64, (c + 1) * 64)
            s1 = [PS(), PS()]
            s2 = [PS(), PS()]
            s3 = PS()
            for hp in range(8):
                for par in range(2):
                    P_ = slice(par * 64, par * 64 + 64)
                    o1 = s1[hp // 4][P_, (hp % 4) * 128:(hp % 4 + 1) * 128]
                    o2 = s2[hp // 4][P_, (hp % 4) * 128:(hp % 4 + 1) * 128]
                    mm(o1, Bt[P_, hp, cs], ARt[P_, hp, c, :])
                    mm(o2, Kt[P_, hp, cs], ARt[P_, hp, c, :])
                    mm(s3[P_, hp * 64:(hp + 1) * 64], ARt[P_, hp, c, 0:64], Bt[P_, hp, cs])
            for q in range(2):
                hs = slice(q * 4, q * 4 + 4)
                tt("dve", SC1[:, hs, :], s1[q][:].rearrange("p (a b) -> p a b", b=128), mask12[:, hs, :], ALU.mult)
                tt("dve", SC2[:, hs, :], s2[q][:].rearrange("p (a b) -> p a b", b=128), mask12[:, hs, :], ALU.mult)
            P0, Q0, G0 = Pb[0], SC1, Gb[0]
            tt("dve", P0[:], s3[:].rearrange("p (a b) -> p a b", b=64), mask3[:], ALU.mult)
            tt("dve", G0[:], SC1[:, :, 0:64], irep[:], ALU.add)
            for (src, dst) in ((Vt, VT), (Bt, BT), (Kt, KT)):
                p = PS()
                for hp in range(8):
                    for par in range(2):
                        P_ = slice(par * 64, par * 64 + 64)
                        mm(p[P_, hp * 64:(hp + 1) * 64], src[P_, hp, cs], identb[P_, par * 64:par * 64 + 64])
                cp("act", dst[:], p[:].rearrange("p (a b) -> p a b", b=64))

            def batch(lhs_fn, rhs_fn):
                p = PS()
                for hp in range(8):
                    for par in range(2):
                        P_ = slice(par * 64, par * 64 + 64)
                        mm(p[P_, hp * 64:(hp + 1) * 64], lhs_fn(P_, hp), rhs_fn(P_, hp))
                return p[:].rearrange("p (a b) -> p a b", b=64)

            def vw(t_):
                return lambda P_, hp: t_[P_, hp, :]

            Qc = lambda P_, hp: SC1[P_, hp, 0:64]
            Pc = vw(Pb[0])
            Gc = Gb[0]
            for lvl in range(1, 6):
                pp = batch(Qc, Pc)
                pq = batch(Pc, Qc) if lvl < 5 else None
                Pn = Pb[lvl % 2]
                cp("act", Pn[:], pp)
                if pq is not None:
                    Qn = Qb[lvl % 2]
                    cp("act", Qn[:], pq)
                    Qc = vw(Qn)
                Pc = vw(Pn)
                pgm = batch(Pc, vw(Gc))
                Gn = Gb[lvl % 2]
                tt("dve", Gn[:], pgm, Gc[:], ALU.add)
                Gc = Gn
            tt("dve", H0p[:], H0[:], WCt[:, :, c:c + 1].to_broadcast([128, 8, 64]), ALU.mult)
            cp("act", H0pb[:], H0p[:])
            p = PS()
            for hp in range(8):
                for par in range(2):
                    P_ = slice(par * 64, par * 64 + 64)
                    o = p[P_, hp * 64:(hp + 1) * 64]
                    mm(o, ARt[P_, hp, c, 0:64], H0pb[P_, hp, :], start=True, stop=False)
                    mm(o, SC2[P_, hp, 0:64], VT[P_, hp, :], start=False, stop=True)
            cp("act", X1b[:], p[:].rearrange("p (a b) -> p a b", b=64))
            pu = batch(lambda P_, hp: Gc[P_, hp, :], lambda P_, hp: X1b[P_, hp, :])
            cp("act", Ub[:], pu)
            py = PS()
            ph = PS()
            for hp in range(8):
                for par in range(2):
                    P_ = slice(par * 64, par * 64 + 64)
                    o = py[P_, hp * 64:(hp + 1) * 64]
                    mm(o, H0pb[P_, hp, :], ARt[P_, hp, c, 64:128], start=True, stop=False)
                    mm(o, Ub[P_, hp, :], SC1[P_, hp, 64:128], start=False, stop=False)
                    mm(o, VT[P_, hp, :], SC2[P_, hp, 64:128], start=False, stop=True)
                    o2 = ph[P_, hp * 64:(hp + 1) * 64]
                    mm(o2, BT[P_, hp, :], Ub[P_, hp, :], start=True, stop=False)
                    mm(o2, KT[P_, hp, :], VT[P_, hp, :], start=False, stop=True)
            cp("act", Y[:, :, cs], py[:].rearrange("p (a b) -> p a b", b=64))
            tt("dve", H0[:], ph[:].rearrange("p (a b) -> p a b", b=64), H0p[:], ALU.add)

        if MST < 5:
            return
        P5N = int(os.environ.get('P5N', '99'))
        for fc in range(8):
            yb_ = nxt("scb", scb)
            ysq = nxt("scb", scb)
            pm = PS()
            pq_ = PS()
            d_ = nxt("scr", scr)[:, 0:T]
            msq = nxt("scr", scr)[:, 0:T]
            var = nxt("scr", scr)[:, 0:T]
            steps = [
                lambda: cp("act", yb_[:], Y[:, fc, :]),
                lambda: mm(pm[:], blkb[:], yb_[:]),
                lambda: act(ysq[:], Y[:, fc, :], AF.Square),
                lambda: mm(pq_[:], blkb[:], ysq[:]),
                lambda: ts("dve", msq, pm[:], 1.0 / 64, None, ALU.mult),
                lambda: tt("dve", d_, Y[:, fc, :], msq, ALU.subtract),
                lambda: act(msq, msq, AF.Square),
                lambda: stt(var, pq_[:], 1.0 / 64, msq, ALU.mult, ALU.subtract),
                lambda: act(var, var, AF.Sqrt, bias=cst[:, 1:2], scale=1.0),
                lambda: recip(var, var),
                lambda: tt("dve", d_, d_, var, ALU.mult),
                lambda: act(d_, d_, AF.Identity, bias=vcol(V_GNB + fc), scale=vcol(V_GNW + fc)),
                lambda: tt("dve", d_, d_, BON[:, fc, :], ALU.add),
                lambda: tt("dve", YA[:, fc, :], d_, GG[:, fc, :], ALU.mult),
            ]
            for st_ in steps[:P5N]:
                st_()

        if MST < 2:
            return
        def cons_pool(j, p, gi):
            c = gi * 2 + j
            wdw = (2, 4, 8, 16)[gi]
            zb = nm["R"]
            sab = (nm["K"], nm["V"])
            cp("dve", zb[:, 1:16], hist[:, c, 1:16])
            cp("act", zb[:, 16:528], p[:])
            cp("dve", hist[:, c, 1:16], zb[:, 513:528])
            prev = zb
            sh = 1
            ia = 0
            while sh < wdw:
                nx_ = sab[ia % 2]
                ia += 1
                lo = 2 * sh
                tt("dve", nx_[:, lo:528], prev[:, lo:528], prev[:, lo - sh:528 - sh], ALU.add)
                prev = nx_
                sh *= 2
            stt(MIX[:, c, :], prev[:, 16:528], 1.0 / wdw, zb[:, 16:528], ALU.mult, ALU.subtract)
            if ti == 0:
                for t_ in range(wdw - 1):
                    stt(MIX[:, c, t_:t_ + 1], prev[:, 16 + t_:17 + t_], 1.0 / (t_ + 1), zb[:, 16 + t_:17 + t_],
                        ALU.mult, ALU.subtract)

        for gi in range(4):
            linear("win", un_, 3360 + gi * 256, 256, (lambda g_: (lambda j, p: cons_pool(j, p, g_)))(gi))
        for gi in range(4):
            for mo in range(2):
                p = PS()
                for ki in range(2):
                    mm(p[:], poolw[:, gi * 2 + ki, mo * 128:(mo + 1) * 128], MIX[:, gi * 2 + ki, :],
                       start=(ki == 0), stop=(ki == 1))
                ts("dve", YB[:, gi * 2 + mo, :], p[:], vcol(V_PS + gi * 2 + mo), None, ALU.mult)

        if MST < 6:
            return
        ya_ = [YA[:, c, :] for c in range(8)]
        yb_c = [YB[:, c, :] for c in range(8)]
        for ob in range(8):
            got = {}

            def mk2(name):
                def c_(j, p):
                    got[(name, j)] = p
                return c_

            linear("pa", ya_, ob * 256, 256, mk2("a"))
            linear("pb", yb_c, ob * 256, 256, mk2("b"))
            linear("win", un_, 4384 + ob * 256, 256, mk2("ga"))
            linear("win", un_, 6432 + ob * 256, 256, mk2("gb"))
            for j in range(2):
                oc = ob * 2 + j
                sa = nxt("scr", scr)[:, 0:T]
                act(sa, got[("ga", j)][:], AF.Sigmoid)
                sb_ = nxt("scr", scr)[:, 0:T]
                act(sb_, got[("gb", j)][:], AF.Sigmoid)
                tt("dve", sa, sa, got[("a", j)][:], ALU.mult)
                tt("dve", sb_, sb_, got[("b", j)][:], ALU.mult)
                tt("dve", Mg[:, oc, :], sa, sb_, ALU.add)

        def cons_o(j, p):
            cp("act", MX[:, j, :], p[:])

        linear("wo", [Mg[:, c, :] for c in range(KC)], 0, D, cons_o)
        postnorm_residual(MX, V_LMPOST, 1.0)

    xs = f_t[:].rearrange("p c t -> p (c t)").rearrange("p (j d) -> p j d", d=D)
    ot_t = nc.alloc_sbuf_tensor_at("ot", [128, 4, D], F32, offset=act_t.manual_sbuf_range[0])
    for ti in range(NT):
        t0 = ti * T
        first_tile[0] = (ti == 0)
        src = x[t0:t0 + T, :].rearrange("(j p) d -> p j d", p=128)
        S.op("sp", (lambda s_: (lambda e: e.dma_start(out=xs, in_=s_)))(src), w=[xs], chan="xl")
        for c in range(KC):
            p = PS()
            for j in range(4):
                tr(p[:, j * 128:(j + 1) * 128], xs[:, j, c * 128:(c + 1) * 128], identf[:])
            cp("act" if c % 2 else "dve", h[:, c, :], p[:])
        if stage >= 1:
            ffn("g1", "u1", "d1", V_L1PRE, V_L1POST)
        if stage >= 2:
            mixer(ti)
        if stage >= 3:
            ffn("g2", "u2", "d2", V_L2PRE, V_L2POST)
        for j in range(4):
            for cb in range(4):
                p = PS()
                for c4 in range(4):
                    c = cb * 4 + c4
                    tr(p[:, c4 * 128:(c4 + 1) * 128], h[:, c, j * 128:(j + 1) * 128], identf[:])
                cp("act" if cb % 2 else "dve", ot_t[:, j, cb * 512:(cb + 1) * 512], p[:])
        dst = out[t0:t0 + T, :].rearrange("(j p) d -> p j d", p=128)
        S.op("sp", (lambda d_: (lambda e: e.dma_start(out=d_, in_=ot_t[:])))(dst), r=[ot_t[:]], chan="os")

    S.finalize()
    st = ExitStack()
    sems = {s: st.enter_context(nc.semaphore("s_" + s)) for s in S.sources}
    block = st.enter_context(nc.Block())
    S.emit(block, sems, final_waits=[("sp", "os")])
    st.close()
    return nc, S


def make_vec(inp):
    v = np.zeros((128, NV), np.float32)

    def put(col, a):
        a = np.asarray(a, np.float32).reshape(-1)
        n = (a.size + 127) // 128
        pad = np.zeros(n * 128, np.float32)
        pad[:a.size] = a
        v[:, col:col + n] = pad.reshape(n, 128).T

    put(V_L1PRE, inp["ln_ffn1_pre"][0])
    put(V_L1POST, inp["ln_ffn1_post"][0])
    put(V_LMPRE, inp["ln_mix_pre"][0])
    put(V_LMPOST, inp["ln_mix_post"][0])
    put(V_L2PRE, inp["ln_ffn2_pre"][0])
    put(V_L2POST, inp["ln_ffn2_post"][0])
    put(V_MU, inp["rwkv_mu"][0])
    put(V_W0, inp["rwkv_w0"][0])
    put(V_A0, inp["rwkv_a0"][0])
    put(V_KK, inp["rwkv_k_k"][0])
    put(V_KA, inp["rwkv_k_a"][0])
    put(V_RK, inp["rwkv_r_k"][0])
    put(V_GNW, inp["rwkv_gn_w"][0])
    put(V_GNB, inp["rwkv_gn_b"][0])
    put(V_PS, inp["pool_scale"][0])
    return v


def shared_inputs(inp):
    f = lambda a: np.ascontiguousarray(np.asarray(a, np.float32))
    m = {
        "g1": f(inp["ffn1_gate"][0]), "u1": f(inp["ffn1_up"][0]), "d1": f(inp["ffn1_down"][0]),
        "win": f(inp["w_in"][0]), "pa": f(inp["w_proj_a"][0]), "pb": f(inp["w_proj_b"][0]),
        "wo": f(inp["w_out"][0]),
        "g2": f(inp["ffn2_gate"][0]), "u2": f(inp["ffn2_up"][0]), "d2": f(inp["ffn2_down"][0]),
        "vec": make_vec(inp),
        "wa2": f(np.concatenate([np.asarray(inp["rwkv_w2"][0]), np.asarray(inp["rwkv_a2"][0])], axis=0)),
        "g2a": f(np.asarray(inp["rwkv_g2"][0])[0:128]),
        "g2b": f(np.asarray(inp["rwkv_g2"][0])[128:160]),
        "poolw": f(np.asarray(inp["pool_w"][0]).reshape(4, 2, 128, 256).transpose(2, 0, 1, 3).reshape(128, 8, 256)),
    }
    return m


_CACHE = {}


def kernel(**inputs):
    NT = SEQ // T
    if NT not in _CACHE:
        _CACHE[NT] = build(NT)
    nc, _ = _CACHE[NT]
    sh = shared_inputs(inputs)
    xfull = np.asarray(inputs["x"], np.float32)
    in_maps = []
    for b in range(NCORES):
        m = dict(sh)
        m["x"] = np.ascontiguousarray(xfull[b])
        in_maps.append(m)
    res = run_bass_kernel_spmd(nc, in_maps, core_ids=list(range(NCORES)))
    return np.stack([np.asarray(r["out"], np.float32) for r in res.results], axis=0)
```

```python
import os
import numpy as np
from contextlib import ExitStack
import concourse.bass as bass
import concourse.mybir as mybir
from concourse.bass_utils import run_bass_kernel_spmd

F32 = mybir.dt.float32
BF16 = mybir.dt.bfloat16
AF = mybir.ActivationFunctionType
ALU = mybir.AluOpType

D = 2048
DFF = 5632
T = 512
KC = D // 128
FC = DFF // 128
INC = 8480
SEQ = 4096
NCORES = 8
SBUF_BASE = 16640
NSLAB = 304
SBUF_BYTES = 229376 - 64

V_L1PRE, V_L1POST, V_LMPRE, V_LMPOST, V_L2PRE, V_L2POST = 0, 16, 32, 48, 64, 80
V_MU, V_W0, V_A0, V_KK, V_KA, V_RK, V_GNW, V_GNB, V_PS = 96, 123, 131, 139, 147, 155, 163, 171, 179
NV = 187


class Op:
    __slots__ = ("eng", "fn", "chan", "deps", "sig", "cnt", "waits", "idx")

    def __init__(self, eng, fn, chan):
        self.eng, self.fn, self.chan = eng, fn, chan
        self.deps = set()
        self.sig = False
        self.cnt = 0
        self.waits = []
        self.idx = -1


def _esize(dt):
    return 2 if dt == BF16 else 4


def ap_keys(ap):
    t = ap.tensor
    tn = type(t).__name__
    pairs = ap.ap
    pst = pairs[0][0]
    npart = pairs[0][1]
    off = ap.offset
    if pst > 0:
        p0 = off // pst
        fo = off % pst
    else:
        p0, fo = 0, off
    halves = []
    if p0 < 64:
        halves.append(0)
    if p0 + npart > 64:
        halves.append(1)
    if tn.startswith("PSum"):
        return [("P", t.name, h) for h in halves]
    lo0 = t.manual_sbuf_range[0]
    es = _esize(ap.dtype)
    ext = 1
    for s, c in pairs[1:]:
        ext += (c - 1) * abs(s)
    lo = lo0 + fo * es
    hi = lo0 + (fo + ext) * es
    ks = []
    for g in range(lo // 256, (hi + 255) // 256):
        for h in halves:
            ks.append((g, h))
    return ks


class Sched:
    ENGS = ("pe", "act", "dve", "pool", "sp")

    def __init__(self):
        self.ops = []
        self.last_w = {}
        self.readers = {}

    def src_of(self, op):
        return op.chan if op.chan is not None else op.eng

    def op(self, eng, fn, r=(), w=(), rk=(), wk=(), chan=None):
        o = Op(eng, fn, chan)
        o.idx = len(self.ops)
        src = self.src_of(o)
        same = (chan is None)
        rkeys = list(rk)
        for a in r:
            rkeys.extend(ap_keys(a))
        wkeys = list(wk)
        for a in w:
            wkeys.extend(ap_keys(a))
        deps = o.deps
        ops = self.ops
        lw = self.last_w
        rd = self.readers
        for k in rkeys:
            x = lw.get(k)
            if x is not None:
                deps.add(x)
        for k in wkeys:
            x = lw.get(k)
            if x is not None:
                a = ops[x]
                if not (same and a.chan is None and a.eng == eng):
                    deps.add(x)
            rs = rd.get(k)
            if rs:
                for s, x in rs.items():
                    if same and s == eng:
                        continue
                    deps.add(x)
        for k in rkeys:
            d = rd.get(k)
            if d is None:
                rd[k] = {src: o.idx}
            else:
                d[src] = o.idx
        for k in wkeys:
            lw[k] = o.idx
            rd[k] = {}
        deps.discard(o.idx)
        ops.append(o)
        return o

    def finalize(self):
        ops = self.ops

        def skip(a, b):
            return a.eng == "pe" and b.eng == "pe" and a.chan is None and b.chan is None

        ordn = {}
        for o in ops:
            s = self.src_of(o)
            ordn[s] = ordn.get(s, 0) + 1
            o.cnt = ordn[s]
        known = {e: {} for e in self.ENGS}
        frozen = {e: {} for e in self.ENGS}
        dirty = {e: False for e in self.ENGS}
        snap = [None] * len(ops)
        needed = set()
        for o in ops:
            kn = known[o.eng]
            need = {}
            for d in o.deps:
                a = ops[d]
                if skip(a, o):
                    continue
                s = self.src_of(a)
                if kn.get(s, 0) >= a.cnt:
                    continue
                if need.get(s, (0, None))[0] < a.cnt:
                    need[s] = (a.cnt, a)
            for s, (c, a) in need.items():
                if kn.get(s, 0) >= c:
                    continue
                o.waits.append(a.idx)
                needed.add(a.idx)
                kn[s] = c
                dirty[o.eng] = True
                base, s_own, c_own = snap[a.idx]
                for s2, c2 in base.items():
                    if kn.get(s2, 0) < c2:
                        kn[s2] = c2
                if s_own is not None and kn.get(s_own, 0) < c_own:
                    kn[s_own] = c_own
            if dirty[o.eng]:
                frozen[o.eng] = dict(kn)
                dirty[o.eng] = False
            snap[o.idx] = (frozen[o.eng], (o.eng if o.chan is None else None), o.cnt)
            o.deps = None
        counts = {}
        for o in ops:
            o.sig = (o.chan is not None) or (o.idx in needed)
            if o.sig:
                s = self.src_of(o)
                counts[s] = counts.get(s, 0) + 1
                o.cnt = counts[s]
        for o in ops:
            o.waits = [(self.src_of(ops[i]), ops[i].cnt) for i in o.waits]
        self.sources = list(counts.keys())
        return counts

    def emit(self, block, sems, final_waits=()):
        per = {e: [] for e in self.ENGS}
        last_cnt = {}
        for o in self.ops:
            per[o.eng].append(o)
            if o.sig:
                last_cnt[self.src_of(o)] = o.cnt
        engs = self.ENGS

        def unit(s):
            return 1 if s in engs else 16

        def run(name, e):
            for o in per[name]:
                for (s, c) in o.waits:
                    e.wait_ge(sems[s], c * unit(s))
                ins = o.fn(e)
                if o.sig:
                    s = self.src_of(o)
                    ins.then_inc(sems[s], unit(s))
            for (en, s) in final_waits:
                if en == name and s in last_cnt:
                    e.wait_ge(sems[s], last_cnt[s] * unit(s))

        block.tensor(lambda e: run("pe", e))
        block.scalar(lambda e: run("act", e))
        block.vector(lambda e: run("dve", e))
        block.gpsimd(lambda e: run("pool", e))
        block.sync(lambda e: run("sp", e))


WSPEC = [("g1", D, DFF), ("u1", D, DFF), ("d1", DFF, D), ("win", D, INC), ("pb", 1024, D),
         ("pa", 1024, D), ("wo", D, D), ("g2", D, DFF), ("u2", D, DFF), ("d2", DFF, D)]


KROWS = {n: K for n, K, M in WSPEC}


def build(NT, stage=3, conv=True, skip=()):
    nc = bass.Bass("TRN2", target_bir_lowering=False)
    S = Sched()
    NTOK = NT * T
    x = nc.dram_tensor("x", [NTOK, D], F32, kind="ExternalInput").ap()
    out = nc.dram_tensor("out", [NTOK, D], F32, kind="ExternalOutput").ap()
    wf, wb = {}, {}
    for n, K, M in WSPEC:
        wf[n] = nc.dram_tensor(n, [K, M], F32, kind="ExternalInput").ap()
    wscs = [nc.dram_tensor("wsc%d" % i, [76, 128, 4096], BF16, kind="Internal").ap() for i in range(4)]
    vec_d = nc.dram_tensor("vec", [128, NV], F32, kind="ExternalInput").ap()
    wa2_d = nc.dram_tensor("wa2", [128, 1024], F32, kind="ExternalInput").ap()
    g2a_d = nc.dram_tensor("g2a", [128, 1024], F32, kind="ExternalInput").ap()
    g2b_d = nc.dram_tensor("g2b", [32, 1024], F32, kind="ExternalInput").ap()
    pw_d = nc.dram_tensor("poolw", [128, 8, 256], F32, kind="ExternalInput").ap()

    cur = [SBUF_BASE]
    cnt = [0]

    def alloc(shape, dt, at=None):
        n = 1
        for s in shape[1:]:
            n *= s
        nb = n * _esize(dt)
        nb = (nb + 63) // 64 * 64
        if at is None:
            if nb >= 1024:
                cur[0] = (cur[0] + 255) // 256 * 256
                nb = (nb + 255) // 256 * 256
            at = cur[0]
            cur[0] += nb
            assert cur[0] <= SBUF_BYTES, ("SBUF overflow", cur[0])
        cnt[0] += 1
        return nc.alloc_sbuf_tensor_at("t%d" % cnt[0], list(shape), dt, offset=at)

    vec = alloc([128, NV], F32)
    cst = alloc([128, 8], F32)
    identf = alloc([128, 128], F32)
    identb = alloc([128, 128], BF16)
    onesb = alloc([128, 128], BF16)
    blkb = alloc([128, 128], BF16)
    mask12 = alloc([128, 8, 128], BF16)
    mask3 = alloc([128, 8, 64], BF16)
    irep = alloc([128, 8, 64], BF16)
    rmask = alloc([128, T], BF16)
    wa2 = alloc([128, 1024], BF16)
    g2a = alloc([128, 1024], BF16)
    g2b = alloc([128, 1024], BF16)
    poolw = alloc([128, 8, 256], BF16)
    zc = alloc([128, 32], F32)
    hist = alloc([128, 8, 16], F32)
    H0 = alloc([128, 8, 64], F32)
    WCt = alloc([128, 8, 8], F32)
    rs = alloc([128, T], F32)
    sqs = [alloc([128, T], BF16) for _ in range(2)]
    h = alloc([128, KC, T], F32)
    xn = alloc([128, KC, T], BF16)
    stg = [alloc([128, 1024], F32) for _ in range(2)]
    NSLOT = 3
    wslots = [alloc([128, 4096], BF16) for _ in range(NSLOT)]
    base = cur[0]
    act_t = alloc([128, FC, T], BF16)
    f_t = alloc([128, KC, T], F32)
    ffn_end = cur[0]
    sgs = [alloc([128, T], F32) for _ in range(2)]
    cur[0] = base
    ARt = alloc([128, 8, 8, 128], BF16)
    Bt = alloc([128, 8, T], BF16)
    Kt = alloc([128, 8, T], BF16)
    Vt = alloc([128, 8, T], BF16)
    BON = alloc([128, 8, T], BF16)
    GG = alloc([128, 8, T], BF16)
    Y = alloc([128, 8, T], F32)
    mix_end = cur[0]
    MIX = alloc([128, 8, T], BF16, at=Kt.manual_sbuf_range[0])
    YB = alloc([128, 8, T], BF16, at=Vt.manual_sbuf_range[0])
    Mg = alloc([128, KC, T], BF16, at=ARt.manual_sbuf_range[0])
    YA = alloc([128, 8, T], BF16, at=Bt.manual_sbuf_range[0])
    MX = alloc([128, KC, T], F32, at=Kt.manual_sbuf_range[0])
    assert Kt.manual_sbuf_range[0] + 32768 <= Y.manual_sbuf_range[0]
    tmp0 = cur[0]
    nm = {k: alloc([128, 528 if k in ("R", "K", "V") else T], F32)
          for k in ("R", "K", "V", "A", "LW", "KKN", "KN", "CUM")}
    nm["E"] = nm["CUM"]
    cumc = alloc([128, 8], F32)
    scr = [alloc([128, 528], F32) for _ in range(4)]
    LAT = alloc([128, T], BF16)
    SGA = alloc([128, T], BF16)
    SGB = alloc([128, T], BF16)
    scb = sqs
    prep_end = cur[0]
    cur[0] = tmp0
    SC1 = alloc([128, 8, 128], BF16)
    SC2 = alloc([128, 8, 128], BF16)
    Pb = [alloc([128, 8, 64], BF16) for _ in range(2)]
    Qb = [alloc([128, 8, 64], BF16) for _ in range(2)]
    Gb = [alloc([128, 8, 64], BF16) for _ in range(2)]
    VT = alloc([128, 8, 64], BF16)
    BT = alloc([128, 8, 64], BF16)
    KT = alloc([128, 8, 64], BF16)
    X1b = alloc([128, 8, 64], BF16)
    Ub = alloc([128, 8, 64], BF16)
    H0p = alloc([128, 8, 64], F32)
    H0pb = alloc([128, 8, 64], BF16)
    cur[0] = max(prep_end, cur[0], ffn_end + 2 * 2048)
    assert cur[0] <= SBUF_BYTES, cur[0]

    psum = [nc.alloc_psum_tensor("ps%d" % i, [128, 512], F32) for i in range(8)]
    pctr = [0]

    def PS():
        p = psum[pctr[0] % 8]
        pctr[0] += 1
        return p

    rot = {"scr": 0, "scb": 0, "sq": 0, "sg": 0, "ws": 0, "stg": 0}

    def nxt(name, lst):
        v = lst[rot[name] % len(lst)]
        rot[name] += 1
        return v

    def mm(o, lhsT, rhs, start=True, stop=True):
        S.op("pe", lambda e: e.matmul(o, lhsT=lhsT, rhs=rhs, start=start, stop=stop), r=[lhsT, rhs], w=[o])

    def tr(o, in_, ident):
        S.op("pe", lambda e: e.transpose(o, in_, ident), r=[in_, ident], w=[o])

    def act(o, in_, func, bias=None, scale=None, eng="act"):
        r = [in_]
        kw = {}
        if bias is not None:
            kw["bias"] = bias
            r.append(bias)
        if scale is not None:
            kw["scale"] = scale
            if not isinstance(scale, float):
                r.append(scale)
        S.op("act", lambda e: e.activation(out=o, in_=in_, func=func, **kw), r=r, w=[o])

    def cp(eng, o, in_):
        if eng == "act":
            S.op("act", lambda e: e.copy(out=o, in_=in_), r=[in_], w=[o])
        else:
            S.op(eng, lambda e: e.tensor_copy(out=o, in_=in_), r=[in_], w=[o])

    def tt(eng, o, a, b, op):
        S.op(eng, lambda e: e.tensor_tensor(out=o, in0=a, in1=b, op=op), r=[a, b], w=[o])

    def ts(eng, o, a, s1, s2, op0, op1=None):
        r = [a] + [s for s in (s1, s2) if s is not None and not isinstance(s, float)]
        if op1 is None:
            S.op(eng, lambda e: e.tensor_scalar(out=o, in0=a, scalar1=s1, scalar2=None, op0=op0), r=r, w=[o])
        else:
            S.op(eng, lambda e: e.tensor_scalar(out=o, in0=a, scalar1=s1, scalar2=s2, op0=op0, op1=op1), r=r, w=[o])

    def stt(o, in0, sc, in1, op0, op1):
        r = [in0, in1] + ([] if isinstance(sc, float) else [sc])
        S.op("dve", lambda e: e.scalar_tensor_tensor(out=o, in0=in0, scalar=sc, in1=in1, op0=op0, op1=op1), r=r, w=[o])

    def recip(o, in_):
        S.op("dve", lambda e: e.reciprocal(out=o, in_=in_), r=[in_], w=[o])

    def memset(eng, o, v):
        S.op(eng, lambda e: e.memset(o, v), w=[o])

    def vcol(c):
        return vec[:, c:c + 1]

    S.op("sp", lambda e: e.dma_start(out=vec[:], in_=vec_d), w=[vec[:]], chan="c0")
    memset("pool", cst[:, 0:1], 1e-6)
    memset("pool", cst[:, 1:2], 64e-5)
    memset("pool", cst[:, 2:3], 0.0)
    memset("pool", identf[:], 1.0)
    S.op("pool", lambda e: e.affine_select(out=identf[:], in_=identf[:], pattern=[[-1, 128]], compare_op=ALU.is_equal,
                                           fill=0.0, base=0, channel_multiplier=1), r=[identf[:]], w=[identf[:]])
    memset("pool", onesb[:], 1.0)
    memset("pool", blkb[:], 0.0)
    memset("pool", blkb[0:64, 0:64], 1.0)
    memset("pool", blkb[64:128, 64:128], 1.0)
    memset("pool", rmask[:], 1.0)
    memset("pool", rmask[:].rearrange("p (c t) -> p c t", t=64)[:, :, 0:1], 0.0)
    memset("pool", zc[:], 0.0)
    memset("pool", hist[:], 0.0)
    memset("pool", H0[:], 0.0)
    mtmp = [nc.alloc_sbuf_tensor_at("mtmp%d" % i, [128, 8, 64], F32, offset=base + i * 2048) for i in range(9)]
    onesf = mtmp[8][:]
    memset("pool", onesf, 1.0)
    mspecs = [(mask12[:, :, 0:64], ALU.is_gt, -1, 1),
              (mask12[:, :, 64:128], ALU.is_ge, -1, 1),
              (mask3[:], ALU.is_gt, 1, -1),
              (irep[:], ALU.is_equal, 1, -1)]

    def asel(o, cmp_, cm, st, base_, last=False):
        S.op("pool", lambda e: e.affine_select(out=o, in_=onesf, pattern=[[0, 8], [st, 64]], compare_op=cmp_,
                                               fill=0.0, base=base_, channel_multiplier=cm), r=[onesf], w=[o],
             wk=(["POOLDONE"] if last else []))

    for i, (dst, cmp_, cm, st) in enumerate(mspecs):
        asel(mtmp[2 * i][:], cmp_, cm, st, 0)
        asel(mtmp[2 * i + 1][:], cmp_, cm, st, -64 * cm, last=(i == 3))
    S.op("dve", lambda e: e.tensor_copy(out=identb[:], in_=identf[:]), r=[identf[:]], w=[identb[:]], rk=["POOLDONE"])
    for i, (dst, cmp_, cm, st) in enumerate(mspecs):
        cp("dve", dst[0:64], mtmp[2 * i][0:64])
        cp("dve", dst[64:128], mtmp[2 * i + 1][64:128])

    first_tile = [True]
    castrr = [0]

    def stage_cast(dst_view, src_ap, np_=128):
        si = rot["stg"] % 2
        rot["stg"] += 1
        st_ = stg[si]
        shp = list(dst_view.shape)
        nel = 1
        for v_ in shp[1:]:
            nel *= v_
        sv = st_[0:np_, 0:nel]
        if len(shp) == 3:
            sv = sv.rearrange("p (c m) -> p c m", m=shp[2])
        S.op("sp", lambda e: e.dma_start(out=sv, in_=src_ap), w=[sv], chan="sg%d" % si)
        castrr[0] += 1
        cp("act" if castrr[0] % 2 else "dve", dst_view, sv)

    slab_ids = {}

    def load_slab(n, k0, nkc, m0, mw):
        slot = nxt("ws", list(range(NSLOT)))
        nel = nkc * mw
        flat = wslots[slot][:, 0:nel]
        view = flat.rearrange("p (c m) -> p c m", m=mw)
        key = ("wb", n, k0, m0)
        if key not in slab_ids:
            slab_ids[key] = len(slab_ids)
            assert len(slab_ids) <= NSLAB
        sid = slab_ids[key]
        dram_b = wscs[sid // 76][sid % 76, :, 0:nel]
        if first_tile[0]:
            npc = 1024 // mw
            for c0 in range(0, nkc, npc):
                n_ = min(npc, nkc - c0)
                src = wf[n][(k0 + c0) * 128:(k0 + c0 + n_) * 128, m0:m0 + mw].rearrange("(c p) m -> p c m", p=128)
                stage_cast(view[:, c0:c0 + n_, :], src)
            S.op("sp", lambda e: e.dma_start(out=dram_b, in_=flat), r=[flat], wk=[key], chan="wst%d" % slot)
        else:
            S.op("sp", lambda e: e.dma_start(out=flat, in_=dram_b), rk=[key], w=[flat], chan="ws%d" % slot)
        return view

    stage_cast(wa2[:], wa2_d)
    stage_cast(g2a[:], g2a_d)
    stage_cast(g2b[0:32, :], g2b_d, np_=32)
    stage_cast(poolw[:, 0:4, :], pw_d[:, 0:4, :])
    stage_cast(poolw[:, 4:8, :], pw_d[:, 4:8, :])

    def linear(n, rhs_chunks, m0, mtot, consume, mw=256):
        nk = len(rhs_chunks)
        parts = [(k0, min(16, nk - k0)) for k0 in range(0, nk, 16)]
        for s0 in range(0, mtot, mw):
            w_ = min(mw, mtot - s0)
            nj = (w_ + 127) // 128
            pss = [PS() for _ in range(nj)]
            for (k0, nkc) in parts:
                slab = load_slab(n, k0, nkc, m0 + s0, w_)
                for j in range(nj):
                    cw = min(128, w_ - j * 128)
                    for kc in range(nkc):
                        mm(pss[j][0:cw, :], slab[:, kc, j * 128:j * 128 + cw], rhs_chunks[k0 + kc],
                           start=(k0 + kc == 0), stop=(k0 + kc == nk - 1))
            for j in range(nj):
                consume(s0 // 128 + j, pss[j])

    def stats(chunks, n_feat):
        p = PS()
        for c, ch in enumerate(chunks):
            sq = nxt("sq", sqs)
            act(sq[:], ch, AF.Square)
            mm(p[:], onesb[:], sq[:], start=(c == 0), stop=(c == len(chunks) - 1))
        act(rs[:], p[:], AF.Sqrt, bias=cst[:, 0:1], scale=1.0 / n_feat)
        recip(rs[:], rs[:])

    def prenorm(gcol):
        stats([h[:, c, :] for c in range(KC)], D)
        for c in range(KC):
            stt(xn[:, c, :], h[:, c, :], vcol(gcol + c), rs[:], ALU.mult, ALU.mult)

    def postnorm_residual(src, gcol, wgt):
        stats([src[:, c, :] for c in range(KC)], D)
        for c in range(KC):
            t1 = nxt("scr", scr)[:, 0:T]
            stt(t1, src[:, c, :], vcol(gcol + c), rs[:], ALU.mult, ALU.mult)
            stt(h[:, c, :], t1, wgt, h[:, c, :], ALU.mult, ALU.add)

    def ffn(gn, un, dn, pre, post):
        prenorm(pre)
        xch = [xn[:, c, :] for c in range(KC)]
        pend = {}

        def cons_g(j, p):
            pend[j] = p

        for jb in range(FC // 2):
            def cons_u(j, p, jb=jb):
                sg = nxt("sg", sgs)
                act(sg[:], pend.pop(j)[:], AF.Silu)
                tt("dve", act_t[:, jb * 2 + j, :], sg[:], p[:], ALU.mult)

            linear(gn, xch, jb * 256, 256, cons_g)
            linear(un, xch, jb * 256, 256, cons_u)

        def cons_d(j, p):
            cp("act", f_t[:, j, :], p[:])

        linear(dn, [act_t[:, c, :] for c in range(FC)], 0, D, cons_d)
        postnorm_residual(f_t, post, 0.5)

    def lerp(p, np_, zidx, o):
        z1 = nxt("scr", scr)
        cp("dve", z1[0:np_, 0:1], zc[0:np_, zidx:zidx + 1])
        cp("act", z1[0:np_, 1:T + 1], p[0:np_, :])
        cp("dve", zc[0:np_, zidx:zidx + 1], z1[0:np_, T:T + 1])
        d = nxt("scr", scr)
        tt("dve", d[0:np_, 0:T], z1[0:np_, 0:T], p[0:np_, :], ALU.subtract)
        stt(o, d[0:np_, 0:T], vec[0:np_, V_MU + zidx:V_MU + zidx + 1], z1[0:np_, 1:T + 1], ALU.mult, ALU.add)

    def c3(ap_):
        return ap_.rearrange("p (c t) -> p c t", t=64)

    def mixer(ti):
        MST = int(os.environ.get('MST', '9'))
        prenorm(V_LMPRE)
        un_ = [xn[:, c, :] for c in range(KC)]
        zl = nm["E"]

        def cons_lat(j, p):
            lerp(p, 128, 24, zl[:])
            act(LAT[0:64, :], zl[0:64, :], AF.Tanh)
            cp("act", LAT[64:128, :], zl[64:128, :])

        linear("win", un_, 3072, 128, cons_lat)

        def cons_gl(j, p):
            if j == 0:
                lerp(p, 128, 25, zl[:])
                act(SGA[:], zl[:], AF.Sigmoid)
            else:
                lerp(p, 32, 26, zl[0:32, :])
                act(SGB[0:32, :], zl[0:32, :], AF.Sigmoid)

        linear("win", un_, 3200, 160, cons_gl)

        if MST < 3:
            return
        R, K_, V_, A_, LW, KKN, KN, CUM, E_ = (nm[k] for k in ("R", "K", "V", "A", "LW", "KKN", "KN", "CUM", "E"))
        R, K_, V_ = R[:, 0:T], K_[:, 0:T], V_[:, 0:T]
        for fc in range(8):
            got = {}

            def mk(name):
                def c_(j, p):
                    got[(name, j)] = p
                return c_

            linear("win", un_, fc * 128, 128, mk("r"), mw=128)
            linear("win", un_, 1024 + fc * 128, 128, mk("k"), mw=128)
            linear("win", un_, 2048 + fc * 128, 128, mk("v"), mw=128)
            for j in range(1):
                cols = slice(fc * 128, (fc + 1) * 128)
                lerp(got[("r", j)], 128, fc, R)
                lerp(got[("k", j)], 128, 8 + fc, K_)
                lerp(got[("v", j)], 128, 16 + fc, V_)
                pw = PS()
                mm(pw[:], wa2[0:64, cols], LAT[0:64, :])
                pa_ = PS()
                mm(pa_[:], wa2[64:128, cols], LAT[64:128, :])
                pg = PS()
                mm(pg[:], g2a[:, cols], SGA[:], start=True, stop=False)
                mm(pg[:], g2b[0:32, cols], SGB[0:32, :], start=False, stop=True)
                act(LW[:], pw[:], AF.Sigmoid, bias=vcol(V_W0 + fc))
                act(A_[:], pa_[:], AF.Sigmoid, bias=vcol(V_A0 + fc))
                cp("act", GG[:, fc, :], pg[:])
                ts("dve", LW[:], LW[:], -0.6065306597126334, None, ALU.mult)
                ts("dve", KKN[:], K_, vcol(V_KK + fc), None, ALU.mult)
                sqb = nxt("scb", scb)
                act(sqb[:], KKN[:], AF.Square)
                pn = PS()
                mm(pn[:], blkb[:], sqb[:])
                t2 = nxt("scr", scr)[:, 0:T]
                act(t2, pn[:], AF.Sqrt, bias=cst[:, 2:3], scale=1.0)
                ts("dve", t2, t2, 1e-12, None, ALU.max)
                recip(t2, t2)
                tt("dve", KKN[:], KKN[:], t2, ALU.mult)
                t3 = nxt("scr", scr)[:, 0:T]
                ts("dve", t3, A_[:], 1.0, vcol(V_KA + fc), ALU.subtract, ALU.mult)
                stt(KN[:], t3, 1.0, K_, ALU.add, ALU.mult)
                rkb = nxt("scb", scb)
                stt(rkb[:], R, vcol(V_RK + fc), KN[:], ALU.mult, ALU.mult)
                pb_ = PS()
                mm(pb_[:], blkb[:], rkb[:])
                tt("dve", BON[:, fc, :], pb_[:], V_, ALU.mult)
                S.op("dve", lambda e: e.tensor_tensor_scan(out=CUM[:], data0=rmask[:], data1=LW[:], initial=0.0,
                                                           op0=ALU.mult, op1=ALU.add), r=[rmask[:], LW[:]], w=[CUM[:]])
                cp("dve", cumc[:], c3(CUM[:])[:, :, 63])
                act(WCt[:, fc, :], cumc[:], AF.Exp)
                tt("dve", c3(E_[:]), cumc[:].unsqueeze(2).to_broadcast([128, 8, 64]), c3(CUM[:]), ALU.subtract)
                t5 = nxt("scr", scr)[:, 0:T]
                act(t5, E_[:], AF.Exp)
                tt("dve", Kt[:, fc, :], KN[:], t5, ALU.mult)
                t6 = nxt("scr", scr)[:, 0:T]
                tt("dve", t6, KKN[:], A_[:], ALU.mult)
                tt("dve", Bt[:, fc, :], t6, t5, ALU.mult)
                t7 = nxt("scr", scr)[:, 0:T]
                act(t7, E_[:], AF.Exp, scale=-1.0)
                tt("dve", ARt[:, fc, :, 64:128], c3(R), c3(t7), ALU.mult)
                t8 = nxt("scr", scr)[:, 0:T]
                tt("dve", t8, E_[:], LW[:], ALU.add)
                act(t8, t8, AF.Exp, scale=-1.0)
                stt(ARt[:, fc, :, 0:64], c3(KKN[:]), -1.0, c3(t8), ALU.mult, ALU.mult)
                cp("act", Vt[:, fc, :], V_)

        if MST < 4:
            return
        for c in range(T // 64):
            cs = slice(c * 64, (c + 1) * 64)
            s1 = [PS(), PS()]
            s2 = [PS(), PS()]
            s3 = PS()
            for hp in range(8):
                for par in range(2):
                    P_ = slice(par * 64, par * 64 + 64)
                    o1 = s1[hp // 4][P_, (hp % 4) * 128:(hp % 4 + 1) * 128]
                    o2 = s2[hp // 4][P_, (hp % 4) * 128:(hp % 4 + 1) * 128]
                    mm(o1, Bt[P_, hp, cs], ARt[P_, hp, c, :])
                    mm(o2, Kt[P_, hp, cs], ARt[P_, hp, c, :])
                    mm(s3[P_, hp * 64:(hp + 1) * 64], ARt[P_, hp, c, 0:64], Bt[P_, hp, cs])
            for q in range(2):
                hs = slice(q * 4, q * 4 + 4)
                tt("dve", SC1[:, hs, :], s1[q][:].rearrange("p (a b) -> p a b", b=128), mask12[:, hs, :], ALU.mult)
                tt("dve", SC2[:, hs, :], s2[q][:].rearrange("p (a b) -> p a b", b=128), mask12[:, hs, :], ALU.mult)
            P0, Q0, G0 = Pb[0], SC1, Gb[0]
            tt("dve", P0[:], s3[:].rearrange("p (a b) -> p a b", b=64), mask3[:], ALU.mult)
            tt("dve", G0[:], SC1[:, :, 0:64], irep[:], ALU.add)
            for (src, dst) in ((Vt, VT), (Bt, BT), (Kt, KT)):
                p = PS()
                for hp in range(8):
                    for par in range(2):
                        P_ = slice(par * 64, par * 64 + 64)
                        mm(p[P_, hp * 64:(hp + 1) * 64], src[P_, hp, cs], identb[P_, par * 64:par * 64 + 64])
                cp("act", dst[:], p[:].rearrange("p (a b) -> p a b", b=64))

            def batch(lhs_fn, rhs_fn):
                p = PS()
                for hp in range(8):
                    for par in range(2):
                        P_ = slice(par * 64, par * 64 + 64)
                        mm(p[P_, hp * 64:(hp + 1) * 64], lhs_fn(P_, hp), rhs_fn(P_, hp))
                return p[:].rearrange("p (a b) -> p a b", b=64)

            def vw(t_):
                return lambda P_, hp: t_[P_, hp, :]

            Qc = lambda P_, hp: SC1[P_, hp, 0:64]
            Pc = vw(Pb[0])
            Gc = Gb[0]
            for lvl in range(1, 6):
                pp = batch(Qc, Pc)
                pq = batch(Pc, Qc) if lvl < 5 else None
                Pn = Pb[lvl % 2]
                cp("act", Pn[:], pp)
                if pq is not None:
                    Qn = Qb[lvl % 2]
                    cp("act", Qn[:], pq)
                    Qc = vw(Qn)
                Pc = vw(Pn)
                pgm = batch(Pc, vw(Gc))
                Gn = Gb[lvl % 2]
                tt("dve", Gn[:], pgm, Gc[:], ALU.add)
                Gc = Gn
            tt("dve", H0p[:], H0[:], WCt[:, :, c:c + 1].to_broadcast([128, 8, 64]), ALU.mult)
            cp("act", H0pb[:], H0p[:])
            p = PS()
            for hp in range(8):
                for par in range(2):
                    P_ = slice(par * 64, par * 64 + 64)
                    o = p[P_, hp * 64:(hp + 1) * 64]
                    mm(o, ARt[P_, hp, c, 0:64], H0pb[P_, hp, :], start=True, stop=False)
                    mm(o, SC2[P_, hp, 0:64], VT[P_, hp, :], start=False, stop=True)
            cp("act", X1b[:], p[:].rearrange("p (a b) -> p a b", b=64))
            pu = batch(lambda P_, hp: Gc[P_, hp, :], lambda P_, hp: X1b[P_, hp, :])
            cp("act", Ub[:], pu)
            py = PS()
            ph = PS()
            for hp in range(8):
                for par in range(2):
                    P_ = slice(par * 64, par * 64 + 64)
                    o = py[P_, hp * 64:(hp + 1) * 64]
                    mm(o, H0pb[P_, hp, :], ARt[P_, hp, c, 64:128], start=True, stop=False)
                    mm(o, Ub[P_, hp, :], SC1[P_, hp, 64:128], start=False, stop=False)
                    mm(o, VT[P_, hp, :], SC2[P_, hp, 64:128], start=False, stop=True)
                    o2 = ph[P_, hp * 64:(hp + 1) * 64]
                    mm(o2, BT[P_, hp, :], Ub[P_, hp, :], start=True, stop=False)
                    mm(o2, KT[P_, hp, :], VT[P_, hp, :], start=False, stop=True)
            cp("act", Y[:, :, cs], py[:].rearrange("p (a b) -> p a b", b=64))
            tt("dve", H0[:], ph[:].rearrange("p (a b) -> p a b", b=64), H0p[:], ALU.add)

        if MST < 5:
            return
        P5N = int(os.environ.get('P5N', '99'))
        for fc in range(8):
            yb_ = nxt("scb", scb)
            ysq = nxt("scb", scb)
            pm = PS()
            pq_ = PS()
            d_ = nxt("scr", scr)[:, 0:T]
            msq = nxt("scr", scr)[:, 0:T]
            var = nxt("scr", scr)[:, 0:T]
            steps = [
                lambda: cp("act", yb_[:], Y[:, fc, :]),
                lambda: mm(pm[:], blkb[:], yb_[:]),
                lambda: act(ysq[:], Y[:, fc, :], AF.Square),
                lambda: mm(pq_[:], blkb[:], ysq[:]),
                lambda: ts("dve", msq, pm[:], 1.0 / 64, None, ALU.mult),
                lambda: tt("dve", d_, Y[:, fc, :], msq, ALU.subtract),
                lambda: act(msq, msq, AF.Square),
                lambda: stt(var, pq_[:], 1.0 / 64, msq, ALU.mult, ALU.subtract),
                lambda: act(var, var, AF.Sqrt, bias=cst[:, 1:2], scale=1.0),
                lambda: recip(var, var),
                lambda: tt("dve", d_, d_, var, ALU.mult),
                lambda: act(d_, d_, AF.Identity, bias=vcol(V_GNB + fc), scale=vcol(V_GNW + fc)),
                lambda: tt("dve", d_, d_, BON[:, fc, :], ALU.add),
                lambda: tt("dve", YA[:, fc, :], d_, GG[:, fc, :], ALU.mult),
            ]
            for st_ in steps[:P5N]:
                st_()

        if MST < 2:
            return
        def cons_pool(j, p, gi):
            c = gi * 2 + j
            wdw = (2, 4, 8, 16)[gi]
            zb = nm["R"]
            sab = (nm["K"], nm["V"])
            cp("dve", zb[:, 1:16], hist[:, c, 1:16])
            cp("act", zb[:, 16:528], p[:])
            cp("dve", hist[:, c, 1:16], zb[:, 513:528])
            prev = zb
            sh = 1
            ia = 0
            while sh < wdw:
                nx_ = sab[ia % 2]
                ia += 1
                lo = 2 * sh
                tt("dve", nx_[:, lo:528], prev[:, lo:528], prev[:, lo - sh:528 - sh], ALU.add)
                prev = nx_
                sh *= 2
            stt(MIX[:, c, :], prev[:, 16:528], 1.0 / wdw, zb[:, 16:528], ALU.mult, ALU.subtract)
            if ti == 0:
                for t_ in range(wdw - 1):
                    stt(MIX[:, c, t_:t_ + 1], prev[:, 16 + t_:17 + t_], 1.0 / (t_ + 1), zb[:, 16 + t_:17 + t_],
                        ALU.mult, ALU.subtract)

        for gi in range(4):
            linear("win", un_, 3360 + gi * 256, 256, (lambda g_: (lambda j, p: cons_pool(j, p, g_)))(gi))
        for gi in range(4):
            for mo in range(2):
                p = PS()
                for ki in range(2):
                    mm(p[:], poolw[:, gi * 2 + ki, mo * 128:(mo + 1) * 128], MIX[:, gi * 2 + ki, :],
                       start=(ki == 0), stop=(ki == 1))
                ts("dve", YB[:, gi * 2 + mo, :], p[:], vcol(V_PS + gi * 2 + mo), None, ALU.mult)

        if MST < 6:
            return
        ya_ = [YA[:, c, :] for c in range(8)]
        yb_c = [YB[:, c, :] for c in range(8)]
        for ob in range(8):
            got = {}

            def mk2(name):
                def c_(j, p):
                    got[(name, j)] = p
                return c_

            linear("pa", ya_, ob * 256, 256, mk2("a"))
            linear("pb", yb_c, ob * 256, 256, mk2("b"))
            linear("win", un_, 4384 + ob * 256, 256, mk2("ga"))
            linear("win", un_, 6432 + ob * 256, 256, mk2("gb"))
            for j in range(2):
                oc = ob * 2 + j
                sa = nxt("scr", scr)[:, 0:T]
                act(sa, got[("ga", j)][:], AF.Sigmoid)
                sb_ = nxt("scr", scr)[:, 0:T]
                act(sb_, got[("gb", j)][:], AF.Sigmoid)
                tt("dve", sa, sa, got[("a", j)][:], ALU.mult)
                tt("dve", sb_, sb_, got[("b", j)][:], ALU.mult)
                tt("dve", Mg[:, oc, :], sa, sb_, ALU.add)

        def cons_o(j, p):
            cp("act", MX[:, j, :], p[:])

        linear("wo", [Mg[:, c, :] for c in range(KC)], 0, D, cons_o)
        postnorm_residual(MX, V_LMPOST, 1.0)

    xs = f_t[:].rearrange("p c t -> p (c t)").rearrange("p (j d) -> p j d", d=D)
    ot_t = nc.alloc_sbuf_tensor_at("ot", [128, 4, D], F32, offset=act_t.manual_sbuf_range[0])
    for ti in range(NT):
        t0 = ti * T
        first_tile[0] = (ti == 0)
        src = x[t0:t0 + T, :].rearrange("(j p) d -> p j d", p=128)
        S.op("sp", (lambda s_: (lambda e: e.dma_start(out=xs, in_=s_)))(src), w=[xs], chan="xl")
        for c in range(KC):
            p = PS()
            for j in range(4):
                tr(p[:, j * 128:(j + 1) * 128], xs[:, j, c * 128:(c + 1) * 128], identf[:])
            cp("act" if c % 2 else "dve", h[:, c, :], p[:])
        if stage >= 1:
            ffn("g1", "u1", "d1", V_L1PRE, V_L1POST)
        if stage >= 2:
            mixer(ti)
        if stage >= 3:
            ffn("g2", "u2", "d2", V_L2PRE, V_L2POST)
        for j in range(4):
            for cb in range(4):
                p = PS()
                for c4 in range(4):
                    c = cb * 4 + c4
                    tr(p[:, c4 * 128:(c4 + 1) * 128], h[:, c, j * 128:(j + 1) * 128], identf[:])
                cp("act" if cb % 2 else "dve", ot_t[:, j, cb * 512:(cb + 1) * 512], p[:])
        dst = out[t0:t0 + T, :].rearrange("(j p) d -> p j d", p=128)
        S.op("sp", (lambda d_: (lambda e: e.dma_start(out=d_, in_=ot_t[:])))(dst), r=[ot_t[:]], chan="os")

    S.finalize()
    st = ExitStack()
    sems = {s: st.enter_context(nc.semaphore("s_" + s)) for s in S.sources}
    block = st.enter_context(nc.Block())
    S.emit(block, sems, final_waits=[("sp", "os")])
    st.close()
    return nc, S


def make_vec(inp):
    v = np.zeros((128, NV), np.float32)

    def put(col, a):
        a = np.asarray(a, np.float32).reshape(-1)
        n = (a.size + 127) // 128
        pad = np.zeros(n * 128, np.float32)
        pad[:a.size] = a
        v[:, col:col + n] = pad.reshape(n, 128).T

    put(V_L1PRE, inp["ln_ffn1_pre"][0])
    put(V_L1POST, inp["ln_ffn1_post"][0])
    put(V_LMPRE, inp["ln_mix_pre"][0])
    put(V_LMPOST, inp["ln_mix_post"][0])
    put(V_L2PRE, inp["ln_ffn2_pre"][0])
    put(V_L2POST, inp["ln_ffn2_post"][0])
    put(V_MU, inp["rwkv_mu"][0])
    put(V_W0, inp["rwkv_w0"][0])
    put(V_A0, inp["rwkv_a0"][0])
    put(V_KK, inp["rwkv_k_k"][0])
    put(V_KA, inp["rwkv_k_a"][0])
    put(V_RK, inp["rwkv_r_k"][0])
    put(V_GNW, inp["rwkv_gn_w"][0])
    put(V_GNB, inp["rwkv_gn_b"][0])
    put(V_PS, inp["pool_scale"][0])
    return v


def shared_inputs(inp):
    f = lambda a: np.ascontiguousarray(np.asarray(a, np.float32))
    m = {
        "g1": f(inp["ffn1_gate"][0]), "u1": f(inp["ffn1_up"][0]), "d1": f(inp["ffn1_down"][0]),
        "win": f(inp["w_in"][0]), "pa": f(inp["w_proj_a"][0]), "pb": f(inp["w_proj_b"][0]),
        "wo": f(inp["w_out"][0]),
        "g2": f(inp["ffn2_gate"][0]), "u2": f(inp["ffn2_up"][0]), "d2": f(inp["ffn2_down"][0]),
        "vec": make_vec(inp),
        "wa2": f(np.concatenate([np.asarray(inp["rwkv_w2"][0]), np.asarray(inp["rwkv_a2"][0])], axis=0)),
        "g2a": f(np.asarray(inp["rwkv_g2"][0])[0:128]),
        "g2b": f(np.asarray(inp["rwkv_g2"][0])[128:160]),
        "poolw": f(np.asarray(inp["pool_w"][0]).reshape(4, 2, 128, 256).transpose(2, 0, 1, 3).reshape(128, 8, 256)),
    }
    return m


_CACHE = {}


def kernel(**inputs):
    NT = SEQ // T
    if NT not in _CACHE:
        _CACHE[NT] = build(NT)
    nc, _ = _CACHE[NT]
    sh = shared_inputs(inputs)
    xfull = np.asarray(inputs["x"], np.float32)
    in_maps = []
    for b in range(NCORES):
        m = dict(sh)
        m["x"] = np.ascontiguousarray(xfull[b])
        in_maps.append(m)
    res = run_bass_kernel_spmd(nc, in_maps, core_ids=list(range(NCORES)))
    return np.stack([np.asarray(r["out"], np.float32) for r in res.results], axis=0)
```

```python
import os
import numpy as np
from contextlib import ExitStack
import concourse.bass as bass
import concourse.mybir as mybir
from concourse.bass_utils import run_bass_kernel_spmd

F32 = mybir.dt.float32
BF16 = mybir.dt.bfloat16
AF = mybir.ActivationFunctionType
ALU = mybir.AluOpType

D = 2048
DFF = 5632
T = 512
KC = D // 128
FC = DFF // 128
INC = 8480
SEQ = 4096
NCORES = 8
SBUF_BASE = 16640
NSLAB = 304
SBUF_BYTES = 229376 - 64

V_L1PRE, V_L1POST, V_LMPRE, V_LMPOST, V_L2PRE, V_L2POST = 0, 16, 32, 48, 64, 80
V_MU, V_W0, V_A0, V_KK, V_KA, V_RK, V_GNW, V_GNB, V_PS = 96, 123, 131, 139, 147, 155, 163, 171, 179
NV = 187


class Op:
    __slots__ = ("eng", "fn", "chan", "deps", "sig", "cnt", "waits", "idx")

    def __init__(self, eng, fn, chan):
        self.eng, self.fn, self.chan = eng, fn, chan
        self.deps = set()
        self.sig = False
        self.cnt = 0
        self.waits = []
        self.idx = -1


def _esize(dt):
    return 2 if dt == BF16 else 4


def ap_keys(ap):
    t = ap.tensor
    tn = type(t).__name__
    pairs = ap.ap
    pst = pairs[0][0]
    npart = pairs[0][1]
    off = ap.offset
    if pst > 0:
        p0 = off // pst
        fo = off % pst
    else:
        p0, fo = 0, off
    halves = []
    if p0 < 64:
        halves.append(0)
    if p0 + npart > 64:
        halves.append(1)
    if tn.startswith("PSum"):
        return [("P", t.name, h) for h in halves]
    lo0 = t.manual_sbuf_range[0]
    es = _esize(ap.dtype)
    ext = 1
    for s, c in pairs[1:]:
        ext += (c - 1) * abs(s)
    lo = lo0 + fo * es
    hi = lo0 + (fo + ext) * es
    ks = []
    for g in range(lo // 256, (hi + 255) // 256):
        for h in halves:
            ks.append((g, h))
    return ks


class Sched:
    ENGS = ("pe", "act", "dve", "pool", "sp")

    def __init__(self):
        self.ops = []
        self.last_w = {}
        self.readers = {}

    def src_of(self, op):
        return op.chan if op.chan is not None else op.eng

    def op(self, eng, fn, r=(), w=(), rk=(), wk=(), chan=None):
        o = Op(eng, fn, chan)
        o.idx = len(self.ops)
        src = self.src_of(o)
        same = (chan is None)
        rkeys = list(rk)
        for a in r:
            rkeys.extend(ap_keys(a))
        wkeys = list(wk)
        for a in w:
            wkeys.extend(ap_keys(a))
        deps = o.deps
        ops = self.ops
        lw = self.last_w
        rd = self.readers
        for k in rkeys:
            x = lw.get(k)
            if x is not None:
                deps.add(x)
        for k in wkeys:
            x = lw.get(k)
            if x is not None:
                a = ops[x]
                if not (same and a.chan is None and a.eng == eng):
                    deps.add(x)
            rs = rd.get(k)
            if rs:
                for s, x in rs.items():
                    if same and s == eng:
                        continue
                    deps.add(x)
        for k in rkeys:
            d = rd.get(k)
            if d is None:
                rd[k] = {src: o.idx}
            else:
                d[src] = o.idx
        for k in wkeys:
            lw[k] = o.idx
            rd[k] = {}
        deps.discard(o.idx)
        ops.append(o)
        return o

    def finalize(self):
        ops = self.ops

        def skip(a, b):
            return a.eng == "pe" and b.eng == "pe" and a.chan is None and b.chan is None

        ordn = {}
        for o in ops:
            s = self.src_of(o)
            ordn[s] = ordn.get(s, 0) + 1
            o.cnt = ordn[s]
        known = {e: {} for e in self.ENGS}
        frozen = {e: {} for e in self.ENGS}
        dirty = {e: False for e in self.ENGS}
        snap = [None] * len(ops)
        needed = set()
        for o in ops:
            kn = known[o.eng]
            need = {}
            for d in o.deps:
                a = ops[d]
                if skip(a, o):
                    continue
                s = self.src_of(a)
                if kn.get(s, 0) >= a.cnt:
                    continue
                if need.get(s, (0, None))[0] < a.cnt:
                    need[s] = (a.cnt, a)
            for s, (c, a) in need.items():
                if kn.get(s, 0) >= c:
                    continue
                o.waits.append(a.idx)
                needed.add(a.idx)
                kn[s] = c
                dirty[o.eng] = True
                base, s_own, c_own = snap[a.idx]
                for s2, c2 in base.items():
                    if kn.get(s2, 0) < c2:
                        kn[s2] = c2
                if s_own is not None and kn.get(s_own, 0) < c_own:
                    kn[s_own] = c_own
            if dirty[o.eng]:
                frozen[o.eng] = dict(kn)
                dirty[o.eng] = False
            snap[o.idx] = (frozen[o.eng], (o.eng if o.chan is None else None), o.cnt)
            o.deps = None
        counts = {}
        for o in ops:
            o.sig = (o.chan is not None) or (o.idx in needed)
            if o.sig:
                s = self.src_of(o)
                counts[s] = counts.get(s, 0) + 1
                o.cnt = counts[s]
        for o in ops:
            o.waits = [(self.src_of(ops[i]), ops[i].cnt) for i in o.waits]
        self.sources = list(counts.keys())
        return counts

    def emit(self, block, sems, final_waits=()):
        per = {e: [] for e in self.ENGS}
        last_cnt = {}
        for o in self.ops:
            per[o.eng].append(o)
            if o.sig:
                last_cnt[self.src_of(o)] = o.cnt
        engs = self.ENGS

        def unit(s):
            return 1 if s in engs else 16

        def run(name, e):
            for o in per[name]:
                for (s, c) in o.waits:
                    e.wait_ge(sems[s], c * unit(s))
                ins = o.fn(e)
                if o.sig:
                    s = self.src_of(o)
                    ins.then_inc(sems[s], unit(s))
            for (en, s) in final_waits:
                if en == name and s in last_cnt:
                    e.wait_ge(sems[s], last_cnt[s] * unit(s))

        block.tensor(lambda e: run("pe", e))
        block.scalar(lambda e: run("act", e))
        block.vector(lambda e: run("dve", e))
        block.gpsimd(lambda e: run("pool", e))
        block.sync(lambda e: run("sp", e))


WSPEC = [("g1", D, DFF), ("u1", D, DFF), ("d1", DFF, D), ("win", D, INC), ("pb", 1024, D),
         ("pa", 1024, D), ("wo", D, D), ("g2", D, DFF), ("u2", D, DFF), ("d2", DFF, D)]


KROWS = {n: K for n, K, M in WSPEC}


def build(NT, stage=3, conv=True, skip=()):
    nc = bass.Bass("TRN2", target_bir_lowering=False)
    S = Sched()
    NTOK = NT * T
    x = nc.dram_tensor("x", [NTOK, D], F32, kind="ExternalInput").ap()
    out = nc.dram_tensor("out", [NTOK, D], F32, kind="ExternalOutput").ap()
    wf, wb = {}, {}
    for n, K, M in WSPEC:
        wf[n] = nc.dram_tensor(n, [K, M], F32, kind="ExternalInput").ap()
    wscs = [nc.dram_tensor("wsc%d" % i, [76, 128, 4096], BF16, kind="Internal").ap() for i in range(4)]
    vec_d = nc.dram_tensor("vec", [128, NV], F32, kind="ExternalInput").ap()
    wa2_d = nc.dram_tensor("wa2", [128, 1024], F32, kind="ExternalInput").ap()
    g2a_d = nc.dram_tensor("g2a", [128, 1024], F32, kind="ExternalInput").ap()
    g2b_d = nc.dram_tensor("g2b", [32, 1024], F32, kind="ExternalInput").ap()
    pw_d = nc.dram_tensor("poolw", [128, 8, 256], F32, kind="ExternalInput").ap()

    cur = [SBUF_BASE]
    cnt = [0]

    def alloc(shape, dt, at=None):
        n = 1
        for s in shape[1:]:
            n *= s
        nb = n * _esize(dt)
        nb = (nb + 63) // 64 * 64
        if at is None:
            if nb >= 1024:
                cur[0] = (cur[0] + 255) // 256 * 256
                nb = (nb + 255) // 256 * 256
            at = cur[0]
            cur[0] += nb
            assert cur[0] <= SBUF_BYTES, ("SBUF overflow", cur[0])
        cnt[0] += 1
        return nc.alloc_sbuf_tensor_at("t%d" % cnt[0], list(shape), dt, offset=at)

    vec = alloc([128, NV], F32)
    cst = alloc([128, 8], F32)
    identf = alloc([128, 128], F32)
    identb = alloc([128, 128], BF16)
    onesb = alloc([128, 128], BF16)
    blkb = alloc([128, 128], BF16)
    mask12 = alloc([128, 8, 128], BF16)
    mask3 = alloc([128, 8, 64], BF16)
    irep = alloc([128, 8, 64], BF16)
    rmask = alloc([128, T], BF16)
    wa2 = alloc([128, 1024], BF16)
    g2a = alloc([128, 1024], BF16)
    g2b = alloc([128, 1024], BF16)
    poolw = alloc([128, 8, 256], BF16)
    zc = alloc([128, 32], F32)
    hist = alloc([128, 8, 16], F32)
    H0 = alloc([128, 8, 64], F32)
    WCt = alloc([128, 8, 8], F32)
    rs = alloc([128, T], F32)
    sqs = [alloc([128, T], BF16) for _ in range(2)]
    h = alloc([128, KC, T], F32)
    xn = alloc([128, KC, T], BF16)
    stg = [alloc([128, 1024], F32) for _ in range(2)]
    NSLOT = 3
    wslots = [alloc([128, 4096], BF16) for _ in range(NSLOT)]
    base = cur[0]
    act_t = alloc([128, FC, T], BF16)
    f_t = alloc([128, KC, T], F32)
    ffn_end = cur[0]
    sgs = [alloc([128, T], F32) for _ in range(2)]
    cur[0] = base
    ARt = alloc([128, 8, 8, 128], BF16)
    Bt = alloc([128, 8, T], BF16)
    Kt = alloc([128, 8, T], BF16)
    Vt = alloc([128, 8, T], BF16)
    BON = alloc([128, 8, T], BF16)
    GG = alloc([128, 8, T], BF16)
    Y = alloc([128, 8, T], F32)
    mix_end = cur[0]
    MIX = alloc([128, 8, T], BF16, at=Kt.manual_sbuf_range[0])
    YB = alloc([128, 8, T], BF16, at=Vt.manual_sbuf_range[0])
    Mg = alloc([128, KC, T], BF16, at=ARt.manual_sbuf_range[0])
    YA = alloc([128, 8, T], BF16, at=Bt.manual_sbuf_range[0])
    MX = alloc([128, KC, T], F32, at=Kt.manual_sbuf_range[0])
    assert Kt.manual_sbuf_range[0] + 32768 <= Y.manual_sbuf_range[0]
    tmp0 = cur[0]
    nm = {k: alloc([128, 528 if k in ("R", "K", "V") else T], F32)
          for k in ("R", "K", "V", "A", "LW", "KKN", "KN", "CUM")}
    nm["E"] = nm["CUM"]
    cumc = alloc([128, 8], F32)
    scr = [alloc([128, 528], F32) for _ in range(4)]
    LAT = alloc([128, T], BF16)
    SGA = alloc([128, T], BF16)
    SGB = alloc([128, T], BF16)
    scb = sqs
    prep_end = cur[0]
    cur[0] = tmp0
    SC1 = alloc([128, 8, 128], BF16)
    SC2 = alloc([128, 8, 128], BF16)
    Pb = [alloc([128, 8, 64], BF16) for _ in range(2)]
    Qb = [alloc([128, 8, 64], BF16) for _ in range(2)]
    Gb = [alloc([128, 8, 64], BF16) for _ in range(2)]
    VT = alloc([128, 8, 64], BF16)
    BT = alloc([128, 8, 64], BF16)
    KT = alloc([128, 8, 64], BF16)
    X1b = alloc([128, 8, 64], BF16)
    Ub = alloc([128, 8, 64], BF16)
    H0p = alloc([128, 8, 64], F32)
    H0pb = alloc([128, 8, 64], BF16)
    cur[0] = max(prep_end, cur[0], ffn_end + 2 * 2048)
    assert cur[0] <= SBUF_BYTES, cur[0]

    psum = [nc.alloc_psum_tensor("ps%d" % i, [128, 512], F32) for i in range(8)]
    pctr = [0]

    def PS():
        p = psum[pctr[0] % 8]
        pctr[0] += 1
        return p

    rot = {"scr": 0, "scb": 0, "sq": 0, "sg": 0, "ws": 0, "stg": 0}

    def nxt(name, lst):
        v = lst[rot[name] % len(lst)]
        rot[name] += 1
        return v

    def mm(o, lhsT, rhs, start=True, stop=True):
        S.op("pe", lambda e: e.matmul(o, lhsT=lhsT, rhs=rhs, start=start, stop=stop), r=[lhsT, rhs], w=[o])

    def tr(o, in_, ident):
        S.op("pe", lambda e: e.transpose(o, in_, ident), r=[in_, ident], w=[o])

    def act(o, in_, func, bias=None, scale=None, eng="act"):
        r = [in_]
        kw = {}
        if bias is not None:
            kw["bias"] = bias
            r.append(bias)
        if scale is not None:
            kw["scale"] = scale
            if not isinstance(scale, float):
                r.append(scale)
        S.op("act", lambda e: e.activation(out=o, in_=in_, func=func, **kw), r=r, w=[o])

    def cp(eng, o, in_):
        if eng == "act":
            S.op("act", lambda e: e.copy(out=o, in_=in_), r=[in_], w=[o])
        else:
            S.op(eng, lambda e: e.tensor_copy(out=o, in_=in_), r=[in_], w=[o])

    def tt(eng, o, a, b, op):
        S.op(eng, lambda e: e.tensor_tensor(out=o, in0=a, in1=b, op=op), r=[a, b], w=[o])

    def ts(eng, o, a, s1, s2, op0, op1=None):
        r = [a] + [s for s in (s1, s2) if s is not None and not isinstance(s, float)]
        if op1 is None:
            S.op(eng, lambda e: e.tensor_scalar(out=o, in0=a, scalar1=s1, scalar2=None, op0=op0), r=r, w=[o])
        else:
            S.op(eng, lambda e: e.tensor_scalar(out=o, in0=a, scalar1=s1, scalar2=s2, op0=op0, op1=op1), r=r, w=[o])

    def stt(o, in0, sc, in1, op0, op1):
        r = [in0, in1] + ([] if isinstance(sc, float) else [sc])
        S.op("dve", lambda e: e.scalar_tensor_tensor(out=o, in0=in0, scalar=sc, in1=in1, op0=op0, op1=op1), r=r, w=[o])

    def recip(o, in_):
        S.op("dve", lambda e: e.reciprocal(out=o, in_=in_), r=[in_], w=[o])

    def memset(eng, o, v):
        S.op(eng, lambda e: e.memset(o, v), w=[o])

    def vcol(c):
        return vec[:, c:c + 1]

    S.op("sp", lambda e: e.dma_start(out=vec[:], in_=vec_d), w=[vec[:]], chan="c0")
    memset("pool", cst[:, 0:1], 1e-6)
    memset("pool", cst[:, 1:2], 64e-5)
    memset("pool", cst[:, 2:3], 0.0)
    memset("pool", identf[:], 1.0)
    S.op("pool", lambda e: e.affine_select(out=identf[:], in_=identf[:], pattern=[[-1, 128]], compare_op=ALU.is_equal,
                                           fill=0.0, base=0, channel_multiplier=1), r=[identf[:]], w=[identf[:]])
    memset("pool", onesb[:], 1.0)
    memset("pool", blkb[:], 0.0)
    memset("pool", blkb[0:64, 0:64], 1.0)
    memset("pool", blkb[64:128, 64:128], 1.0)
    memset("pool", rmask[:], 1.0)
    memset("pool", rmask[:].rearrange("p (c t) -> p c t", t=64)[:, :, 0:1], 0.0)
    memset("pool", zc[:], 0.0)
    memset("pool", hist[:], 0.0)
    memset("pool", H0[:], 0.0)
    mtmp = [nc.alloc_sbuf_tensor_at("mtmp%d" % i, [128, 8, 64], F32, offset=base + i * 2048) for i in range(9)]
    onesf = mtmp[8][:]
    memset("pool", onesf, 1.0)
    mspecs = [(mask12[:, :, 0:64], ALU.is_gt, -1, 1),
              (mask12[:, :, 64:128], ALU.is_ge, -1, 1),
              (mask3[:], ALU.is_gt, 1, -1),
              (irep[:], ALU.is_equal, 1, -1)]

    def asel(o, cmp_, cm, st, base_, last=False):
        S.op("pool", lambda e: e.affine_select(out=o, in_=onesf, pattern=[[0, 8], [st, 64]], compare_op=cmp_,
                                               fill=0.0, base=base_, channel_multiplier=cm), r=[onesf], w=[o],
             wk=(["POOLDONE"] if last else []))

    for i, (dst, cmp_, cm, st) in enumerate(mspecs):
        asel(mtmp[2 * i][:], cmp_, cm, st, 0)
        asel(mtmp[2 * i + 1][:], cmp_, cm, st, -64 * cm, last=(i == 3))
    S.op("dve", lambda e: e.tensor_copy(out=identb[:], in_=identf[:]), r=[identf[:]], w=[identb[:]], rk=["POOLDONE"])
    for i, (dst, cmp_, cm, st) in enumerate(mspecs):
        cp("dve", dst[0:64], mtmp[2 * i][0:64])
        cp("dve", dst[64:128], mtmp[2 * i + 1][64:128])

    first_tile = [True]
    castrr = [0]

    def stage_cast(dst_view, src_ap, np_=128):
        si = rot["stg"] % 2
        rot["stg"] += 1
        st_ = stg[si]
        shp = list(dst_view.shape)
        nel = 1
        for v_ in shp[1:]:
            nel *= v_
        sv = st_[0:np_, 0:nel]
        if len(shp) == 3:
            sv = sv.rearrange("p (c m) -> p c m", m=shp[2])
        S.op("sp", lambda e: e.dma_start(out=sv, in_=src_ap), w=[sv], chan="sg%d" % si)
        castrr[0] += 1
        cp("act" if castrr[0] % 2 else "dve", dst_view, sv)

    slab_ids = {}
    pend_store = [None]

    def load_slab(n, k0, nkc, m0, mw):
        slot = nxt("ws", list(range(NSLOT)))
        nel = nkc * mw
        flat = wslots[slot][:, 0:nel]
        view = flat.rearrange("p (c m) -> p c m", m=mw)
        key = ("wb", n, k0, m0)
        if key not in slab_ids:
            slab_ids[key] = len(slab_ids)
            assert len(slab_ids) <= NSLAB
        sid = slab_ids[key]
        dram_b = wscs[sid // 76][sid % 76, :, 0:nel]
        if first_tile[0]:
            npc = 1024 // mw
            for c0 in range(0, nkc, npc):
                n_ = min(npc, nkc - c0)
                src = wf[n][(k0 + c0) * 128:(k0 + c0 + n_) * 128, m0:m0 + mw].rearrange("(c p) m -> p c m", p=128)
                stage_cast(view[:, c0:c0 + n_, :], src)
            if pend_store[0] is not None:
                pend_store[0]()
            pend_store[0] = lambda: S.op("sp", lambda e: e.dma_start(out=dram_b, in_=flat), r=[flat], wk=[key],
                                         chan="wst%d" % slot)
        else:
            S.op("sp", lambda e: e.dma_start(out=flat, in_=dram_b), rk=[key], w=[flat], chan="ws%d" % slot)
        return view

    stage_cast(wa2[:], wa2_d)
    stage_cast(g2a[:], g2a_d)
    stage_cast(g2b[0:32, :], g2b_d, np_=32)
    stage_cast(poolw[:, 0:4, :], pw_d[:, 0:4, :])
    stage_cast(poolw[:, 4:8, :], pw_d[:, 4:8, :])

    def linear(n, rhs_chunks, m0, mtot, consume, mw=256):
        nk = len(rhs_chunks)
        parts = [(k0, min(16, nk - k0)) for k0 in range(0, nk, 16)]
        for s0 in range(0, mtot, mw):
            w_ = min(mw, mtot - s0)
            nj = (w_ + 127) // 128
            pss = [PS() for _ in range(nj)]
            for (k0, nkc) in parts:
                slab = load_slab(n, k0, nkc, m0 + s0, w_)
                for j in range(nj):
                    cw = min(128, w_ - j * 128)
                    for kc in range(nkc):
                        mm(pss[j][0:cw, :], slab[:, kc, j * 128:j * 128 + cw], rhs_chunks[k0 + kc],
                           start=(k0 + kc == 0), stop=(k0 + kc == nk - 1))
            for j in range(nj):
                consume(s0 // 128 + j, pss[j])

    def stats(chunks, n_feat):
        p = PS()
        for c, ch in enumerate(chunks):
            sq = nxt("sq", sqs)
            if c % 2 == 0:
                act(sq[:], ch, AF.Square)
            else:
                tt("dve", sq[:], ch, ch, ALU.mult)
            mm(p[:], onesb[:], sq[:], start=(c == 0), stop=(c == len(chunks) - 1))
        act(rs[:], p[:], AF.Sqrt, bias=cst[:, 0:1], scale=1.0 / n_feat)
        recip(rs[:], rs[:])

    def prenorm(gcol):
        stats([h[:, c, :] for c in range(KC)], D)
        for c in range(KC):
            stt(xn[:, c, :], h[:, c, :], vcol(gcol + c), rs[:], ALU.mult, ALU.mult)

    def postnorm_residual(src, gcol, wgt):
        stats([src[:, c, :] for c in range(KC)], D)
        for c in range(KC):
            t1 = nxt("scr", scr)[:, 0:T]
            stt(t1, src[:, c, :], vcol(gcol + c), rs[:], ALU.mult, ALU.mult)
            stt(h[:, c, :], t1, wgt, h[:, c, :], ALU.mult, ALU.add)

    def ffn(gn, un, dn, pre, post):
        prenorm(pre)
        xch = [xn[:, c, :] for c in range(KC)]
        pend = {}

        def cons_g(j, p):
            pend[j] = p

        for jb in range(FC // 2):
            def cons_u(j, p, jb=jb):
                sg = nxt("sg", sgs)
                act(sg[:], pend.pop(j)[:], AF.Silu)
                tt("dve", act_t[:, jb * 2 + j, :], sg[:], p[:], ALU.mult)

            linear(gn, xch, jb * 256, 256, cons_g)
            linear(un, xch, jb * 256, 256, cons_u)

        def cons_d(j, p):
            cp("act", f_t[:, j, :], p[:])

        linear(dn, [act_t[:, c, :] for c in range(FC)], 0, D, cons_d)
        postnorm_residual(f_t, post, 0.5)

    def lerp(p, np_, zidx, o):
        z1 = nxt("scr", scr)
        cp("dve", z1[0:np_, 0:1], zc[0:np_, zidx:zidx + 1])
        cp("act", z1[0:np_, 1:T + 1], p[0:np_, :])
        cp("dve", zc[0:np_, zidx:zidx + 1], z1[0:np_, T:T + 1])
        d = nxt("scr", scr)
        tt("dve", d[0:np_, 0:T], z1[0:np_, 0:T], p[0:np_, :], ALU.subtract)
        stt(o, d[0:np_, 0:T], vec[0:np_, V_MU + zidx:V_MU + zidx + 1], z1[0:np_, 1:T + 1], ALU.mult, ALU.add)

    def c3(ap_):
        return ap_.rearrange("p (c t) -> p c t", t=64)

    def mixer(ti):
        MST = int(os.environ.get('MST', '9'))
        prenorm(V_LMPRE)
        un_ = [xn[:, c, :] for c in range(KC)]
        zl = nm["E"]

        def cons_lat(j, p):
            lerp(p, 128, 24, zl[:])
            act(LAT[0:64, :], zl[0:64, :], AF.Tanh)
            cp("act", LAT[64:128, :], zl[64:128, :])

        linear("win", un_, 3072, 128, cons_lat)

        def cons_gl(j, p):
            if j == 0:
                lerp(p, 128, 25, zl[:])
                act(SGA[:], zl[:], AF.Sigmoid)
            else:
                lerp(p, 32, 26, zl[0:32, :])
                act(SGB[0:32, :], zl[0:32, :], AF.Sigmoid)

        linear("win", un_, 3200, 160, cons_gl)

        if MST < 3:
            return
        R, K_, V_, A_, LW, KKN, KN, CUM, E_ = (nm[k] for k in ("R", "K", "V", "A", "LW", "KKN", "KN", "CUM", "E"))
        R, K_, V_ = R[:, 0:T], K_[:, 0:T], V_[:, 0:T]
        for fc in range(8):
            got = {}

            def mk(name):
                def c_(j, p):
                    got[(name, j)] = p
                return c_

            linear("win", un_, fc * 128, 128, mk("r"), mw=128)
            linear("win", un_, 1024 + fc * 128, 128, mk("k"), mw=128)
            linear("win", un_, 2048 + fc * 128, 128, mk("v"), mw=128)
            for j in range(1):
                cols = slice(fc * 128, (fc + 1) * 128)
                lerp(got[("r", j)], 128, fc, R)
                lerp(got[("k", j)], 128, 8 + fc, K_)
                lerp(got[("v", j)], 128, 16 + fc, V_)
                pw = PS()
                mm(pw[:], wa2[0:64, cols], LAT[0:64, :])
                pa_ = PS()
                mm(pa_[:], wa2[64:128, cols], LAT[64:128, :])
                pg = PS()
                mm(pg[:], g2a[:, cols], SGA[:], start=True, stop=False)
                mm(pg[:], g2b[0:32, cols], SGB[0:32, :], start=False, stop=True)
                act(LW[:], pw[:], AF.Sigmoid, bias=vcol(V_W0 + fc))
                act(A_[:], pa_[:], AF.Sigmoid, bias=vcol(V_A0 + fc))
                cp("act", GG[:, fc, :], pg[:])
                ts("dve", LW[:], LW[:], -0.6065306597126334, None, ALU.mult)
                ts("dve", KKN[:], K_, vcol(V_KK + fc), None, ALU.mult)
                sqb = nxt("scb", scb)
                act(sqb[:], KKN[:], AF.Square)
                pn = PS()
                mm(pn[:], blkb[:], sqb[:])
                t2 = nxt("scr", scr)[:, 0:T]
                act(t2, pn[:], AF.Sqrt, bias=cst[:, 2:3], scale=1.0)
                ts("dve", t2, t2, 1e-12, None, ALU.max)
                recip(t2, t2)
                tt("dve", KKN[:], KKN[:], t2, ALU.mult)
                t3 = nxt("scr", scr)[:, 0:T]
                ts("dve", t3, A_[:], 1.0, vcol(V_KA + fc), ALU.subtract, ALU.mult)
                stt(KN[:], t3, 1.0, K_, ALU.add, ALU.mult)
                rkb = nxt("scb", scb)
                stt(rkb[:], R, vcol(V_RK + fc), KN[:], ALU.mult, ALU.mult)
                pb_ = PS()
                mm(pb_[:], blkb[:], rkb[:])
                tt("dve", BON[:, fc, :], pb_[:], V_, ALU.mult)
                S.op("dve", lambda e: e.tensor_tensor_scan(out=CUM[:], data0=rmask[:], data1=LW[:], initial=0.0,
                                                           op0=ALU.mult, op1=ALU.add), r=[rmask[:], LW[:]], w=[CUM[:]])
                cp("dve", cumc[:], c3(CUM[:])[:, :, 63])
                act(WCt[:, fc, :], cumc[:], AF.Exp)
                tt("dve", c3(E_[:]), cumc[:].unsqueeze(2).to_broadcast([128, 8, 64]), c3(CUM[:]), ALU.subtract)
                t5 = nxt("scr", scr)[:, 0:T]
                act(t5, E_[:], AF.Exp)
                tt("dve", Kt[:, fc, :], KN[:], t5, ALU.mult)
                t6 = nxt("scr", scr)[:, 0:T]
                tt("dve", t6, KKN[:], A_[:], ALU.mult)
                tt("dve", Bt[:, fc, :], t6, t5, ALU.mult)
                t7 = nxt("scr", scr)[:, 0:T]
                act(t7, E_[:], AF.Exp, scale=-1.0)
                tt("dve", ARt[:, fc, :, 64:128], c3(R), c3(t7), ALU.mult)
                t8 = nxt("scr", scr)[:, 0:T]
                tt("dve", t8, E_[:], LW[:], ALU.add)
                act(t8, t8, AF.Exp, scale=-1.0)
                stt(ARt[:, fc, :, 0:64], c3(KKN[:]), -1.0, c3(t8), ALU.mult, ALU.mult)
                cp("act", Vt[:, fc, :], V_)

        if MST < 4:
            return
        for c in range(T // 64):
            cs = slice(c * 64, (c + 1) * 64)
            s1 = [PS(), PS()]
            s2 = [PS(), PS()]
            s3 = PS()
            for hp in range(8):
                for par in range(2):
                    P_ = slice(par * 64, par * 64 + 64)
                    o1 = s1[hp // 4][P_, (hp % 4) * 128:(hp % 4 + 1) * 128]
                    o2 = s2[hp // 4][P_, (hp % 4) * 128:(hp % 4 + 1) * 128]
                    mm(o1, Bt[P_, hp, cs], ARt[P_, hp, c, :])
                    mm(o2, Kt[P_, hp, cs], ARt[P_, hp, c, :])
                    mm(s3[P_, hp * 64:(hp + 1) * 64], ARt[P_, hp, c, 0:64], Bt[P_, hp, cs])
            for q in range(2):
                hs = slice(q * 4, q * 4 + 4)
                tt("dve", SC1[:, hs, :], s1[q][:].rearrange("p (a b) -> p a b", b=128), mask12[:, hs, :], ALU.mult)
                tt("dve", SC2[:, hs, :], s2[q][:].rearrange("p (a b) -> p a b", b=128), mask12[:, hs, :], ALU.mult)
            P0, Q0, G0 = Pb[0], SC1, Gb[0]
            tt("dve", P0[:], s3[:].rearrange("p (a b) -> p a b", b=64), mask3[:], ALU.mult)
            tt("dve", G0[:], SC1[:, :, 0:64], irep[:], ALU.add)
            for (src, dst) in ((Vt, VT), (Bt, BT), (Kt, KT)):
                p = PS()
                for hp in range(8):
                    for par in range(2):
                        P_ = slice(par * 64, par * 64 + 64)
                        mm(p[P_, hp * 64:(hp + 1) * 64], src[P_, hp, cs], identb[P_, par * 64:par * 64 + 64])
                cp("act", dst[:], p[:].rearrange("p (a b) -> p a b", b=64))

            def batch(lhs_fn, rhs_fn):
                p = PS()
                for hp in range(8):
                    for par in range(2):
                        P_ = slice(par * 64, par * 64 + 64)
                        mm(p[P_, hp * 64:(hp + 1) * 64], lhs_fn(P_, hp), rhs_fn(P_, hp))
                return p[:].rearrange("p (a b) -> p a b", b=64)

            def vw(t_):
                return lambda P_, hp: t_[P_, hp, :]

            Qc = lambda P_, hp: SC1[P_, hp, 0:64]
            Pc = vw(Pb[0])
            Gc = Gb[0]
            for lvl in range(1, 6):
                pp = batch(Qc, Pc)
                pq = batch(Pc, Qc) if lvl < 5 else None
                Pn = Pb[lvl % 2]
                cp("act", Pn[:], pp)
                if pq is not None:
                    Qn = Qb[lvl % 2]
                    cp("act", Qn[:], pq)
                    Qc = vw(Qn)
                Pc = vw(Pn)
                pgm = batch(Pc, vw(Gc))
                Gn = Gb[lvl % 2]
                tt("dve", Gn[:], pgm, Gc[:], ALU.add)
                Gc = Gn
            tt("dve", H0p[:], H0[:], WCt[:, :, c:c + 1].to_broadcast([128, 8, 64]), ALU.mult)
            cp("act", H0pb[:], H0p[:])
            p = PS()
            for hp in range(8):
                for par in range(2):
                    P_ = slice(par * 64, par * 64 + 64)
                    o = p[P_, hp * 64:(hp + 1) * 64]
                    mm(o, ARt[P_, hp, c, 0:64], H0pb[P_, hp, :], start=True, stop=False)
                    mm(o, SC2[P_, hp, 0:64], VT[P_, hp, :], start=False, stop=True)
            cp("act", X1b[:], p[:].rearrange("p (a b) -> p a b", b=64))
            pu = batch(lambda P_, hp: Gc[P_, hp, :], lambda P_, hp: X1b[P_, hp, :])
            cp("act", Ub[:], pu)
            py = PS()
            ph = PS()
            for hp in range(8):
                for par in range(2):
                    P_ = slice(par * 64, par * 64 + 64)
                    o = py[P_, hp * 64:(hp + 1) * 64]
                    mm(o, H0pb[P_, hp, :], ARt[P_, hp, c, 64:128], start=True, stop=False)
                    mm(o, Ub[P_, hp, :], SC1[P_, hp, 64:128], start=False, stop=False)
                    mm(o, VT[P_, hp, :], SC2[P_, hp, 64:128], start=False, stop=True)
                    o2 = ph[P_, hp * 64:(hp + 1) * 64]
                    mm(o2, BT[P_, hp, :], Ub[P_, hp, :], start=True, stop=False)
                    mm(o2, KT[P_, hp, :], VT[P_, hp, :], start=False, stop=True)
            cp("act", Y[:, :, cs], py[:].rearrange("p (a b) -> p a b", b=64))
            tt("dve", H0[:], ph[:].rearrange("p (a b) -> p a b", b=64), H0p[:], ALU.add)

        if MST < 5:
            return
        P5N = int(os.environ.get('P5N', '99'))
        for fc in range(8):
            yb_ = nxt("scb", scb)
            ysq = nxt("scb", scb)
            pm = PS()
            pq_ = PS()
            d_ = nxt("scr", scr)[:, 0:T]
            msq = nxt("scr", scr)[:, 0:T]
            var = nxt("scr", scr)[:, 0:T]
            steps = [
                lambda: cp("act", yb_[:], Y[:, fc, :]),
                lambda: mm(pm[:], blkb[:], yb_[:]),
                lambda: act(ysq[:], Y[:, fc, :], AF.Square),
                lambda: mm(pq_[:], blkb[:], ysq[:]),
                lambda: ts("dve", msq, pm[:], 1.0 / 64, None, ALU.mult),
                lambda: tt("dve", d_, Y[:, fc, :], msq, ALU.subtract),
                lambda: act(msq, msq, AF.Square),
                lambda: stt(var, pq_[:], 1.0 / 64, msq, ALU.mult, ALU.subtract),
                lambda: act(var, var, AF.Sqrt, bias=cst[:, 1:2], scale=1.0),
                lambda: recip(var, var),
                lambda: tt("dve", d_, d_, var, ALU.mult),
                lambda: act(d_, d_, AF.Identity, bias=vcol(V_GNB + fc), scale=vcol(V_GNW + fc)),
                lambda: tt("dve", d_, d_, BON[:, fc, :], ALU.add),
                lambda: tt("dve", YA[:, fc, :], d_, GG[:, fc, :], ALU.mult),
            ]
            for st_ in steps[:P5N]:
                st_()

        if MST < 2:
            return
        def cons_pool(j, p, gi):
            c = gi * 2 + j
            wdw = (2, 4, 8, 16)[gi]
            zb = nm["R"]
            sab = (nm["K"], nm["V"])
            cp("dve", zb[:, 1:16], hist[:, c, 1:16])
            cp("act", zb[:, 16:528], p[:])
            cp("dve", hist[:, c, 1:16], zb[:, 513:528])
            prev = zb
            sh = 1
            ia = 0
            while sh < wdw:
                nx_ = sab[ia % 2]
                ia += 1
                lo = 2 * sh
                tt("dve", nx_[:, lo:528], prev[:, lo:528], prev[:, lo - sh:528 - sh], ALU.add)
                prev = nx_
                sh *= 2
            stt(MIX[:, c, :], prev[:, 16:528], 1.0 / wdw, zb[:, 16:528], ALU.mult, ALU.subtract)
            if ti == 0:
                for t_ in range(wdw - 1):
                    stt(MIX[:, c, t_:t_ + 1], prev[:, 16 + t_:17 + t_], 1.0 / (t_ + 1), zb[:, 16 + t_:17 + t_],
                        ALU.mult, ALU.subtract)

        for gi in range(4):
            linear("win", un_, 3360 + gi * 256, 256, (lambda g_: (lambda j, p: cons_pool(j, p, g_)))(gi))
        for gi in range(4):
            for mo in range(2):
                p = PS()
                for ki in range(2):
                    mm(p[:], poolw[:, gi * 2 + ki, mo * 128:(mo + 1) * 128], MIX[:, gi * 2 + ki, :],
                       start=(ki == 0), stop=(ki == 1))
                ts("dve", YB[:, gi * 2 + mo, :], p[:], vcol(V_PS + gi * 2 + mo), None, ALU.mult)

        if MST < 6:
            return
        ya_ = [YA[:, c, :] for c in range(8)]
        yb_c = [YB[:, c, :] for c in range(8)]
        for ob in range(8):
            got = {}

            def mk2(name):
                def c_(j, p):
                    got[(name, j)] = p
                return c_

            linear("pa", ya_, ob * 256, 256, mk2("a"))
            linear("pb", yb_c, ob * 256, 256, mk2("b"))
            linear("win", un_, 4384 + ob * 256, 256, mk2("ga"))
            linear("win", un_, 6432 + ob * 256, 256, mk2("gb"))
            for j in range(2):
                oc = ob * 2 + j
                sa = nxt("scr", scr)[:, 0:T]
                act(sa, got[("ga", j)][:], AF.Sigmoid)
                sb_ = nxt("scr", scr)[:, 0:T]
                act(sb_, got[("gb", j)][:], AF.Sigmoid)
                tt("dve", sa, sa, got[("a", j)][:], ALU.mult)
                tt("dve", sb_, sb_, got[("b", j)][:], ALU.mult)
                tt("dve", Mg[:, oc, :], sa, sb_, ALU.add)

        def cons_o(j, p):
            cp("act", MX[:, j, :], p[:])

        linear("wo", [Mg[:, c, :] for c in range(KC)], 0, D, cons_o)
        postnorm_residual(MX, V_LMPOST, 1.0)

    xs = f_t[:].rearrange("p c t -> p (c t)").rearrange("p (j d) -> p j d", d=D)
    ot_t = nc.alloc_sbuf_tensor_at("ot", [128, 4, D], F32, offset=act_t.manual_sbuf_range[0])
    def load_x(ti_):
        for j in range(4):
            src = x[ti_ * T + j * 128:ti_ * T + (j + 1) * 128, :]
            S.op("sp", (lambda s_, j_: (lambda e: e.dma_start(out=xs[:, j_, :], in_=s_)))(src, j), w=[xs[:, j, :]],
                 chan="xl%d" % j)

    load_x(0)
    for ti in range(NT):
        t0 = ti * T
        first_tile[0] = (ti == 0)
        for c in range(KC):
            p = PS()
            for j in range(4):
                tr(p[:, j * 128:(j + 1) * 128], xs[:, j, c * 128:(c + 1) * 128], identf[:])
            cp("act" if c % 2 else "dve", h[:, c, :], p[:])
        if stage >= 1:
            ffn("g1", "u1", "d1", V_L1PRE, V_L1POST)
        if stage >= 2:
            mixer(ti)
        if stage >= 3:
            ffn("g2", "u2", "d2", V_L2PRE, V_L2POST)
        if pend_store[0] is not None:
            pend_store[0]()
            pend_store[0] = None
        if ti + 1 < NT:
            load_x(ti + 1)
        for j in range(4):
            for cb in range(4):
                p = PS()
                for c4 in range(4):
                    c = cb * 4 + c4
                    tr(p[:, c4 * 128:(c4 + 1) * 128], h[:, c, j * 128:(j + 1) * 128], identf[:])
                cp("act" if cb % 2 else "dve", ot_t[:, j, cb * 512:(cb + 1) * 512], p[:])
        dst = out[t0:t0 + T, :].rearrange("(j p) d -> p j d", p=128)
        S.op("sp", (lambda d_: (lambda e: e.dma_start(out=d_, in_=ot_t[:])))(dst), r=[ot_t[:]], chan="os")

    S.finalize()
    st = ExitStack()
    sems = {s: st.enter_context(nc.semaphore("s_" + s)) for s in S.sources}
    block = st.enter_context(nc.Block())
    S.emit(block, sems, final_waits=[("sp", "os")])
    st.close()
    return nc, S


def make_vec(inp):
    v = np.zeros((128, NV), np.float32)

    def put(col, a):
        a = np.asarray(a, np.float32).reshape(-1)
        n = (a.size + 127) // 128
        pad = np.zeros(n * 128, np.float32)
        pad[:a.size] = a
        v[:, col:col + n] = pad.reshape(n, 128).T

    put(V_L1PRE, inp["ln_ffn1_pre"][0])
    put(V_L1POST, inp["ln_ffn1_post"][0])
    put(V_LMPRE, inp["ln_mix_pre"][0])
    put(V_LMPOST, inp["ln_mix_post"][0])
    put(V_L2PRE, inp["ln_ffn2_pre"][0])
    put(V_L2POST, inp["ln_ffn2_post"][0])
    put(V_MU, inp["rwkv_mu"][0])
    put(V_W0, inp["rwkv_w0"][0])
    put(V_A0, inp["rwkv_a0"][0])
    put(V_KK, inp["rwkv_k_k"][0])
    put(V_KA, inp["rwkv_k_a"][0])
    put(V_RK, inp["rwkv_r_k"][0])
    put(V_GNW, inp["rwkv_gn_w"][0])
    put(V_GNB, inp["rwkv_gn_b"][0])
    put(V_PS, inp["pool_scale"][0])
    return v


def shared_inputs(inp):
    f = lambda a: np.ascontiguousarray(np.asarray(a, np.float32))
    m = {
        "g1": f(inp["ffn1_gate"][0]), "u1": f(inp["ffn1_up"][0]), "d1": f(inp["ffn1_down"][0]),
        "win": f(inp["w_in"][0]), "pa": f(inp["w_proj_a"][0]), "pb": f(inp["w_proj_b"][0]),
        "wo": f(inp["w_out"][0]),
        "g2": f(inp["ffn2_gate"][0]), "u2": f(inp["ffn2_up"][0]), "d2": f(inp["ffn2_down"][0]),
        "vec": make_vec(inp),
        "wa2": f(np.concatenate([np.asarray(inp["rwkv_w2"][0]), np.asarray(inp["rwkv_a2"][0])], axis=0)),
        "g2a": f(np.asarray(inp["rwkv_g2"][0])[0:128]),
        "g2b": f(np.asarray(inp["rwkv_g2"][0])[128:160]),
        "poolw": f(np.asarray(inp["pool_w"][0]).reshape(4, 2, 128, 256).transpose(2, 0, 1, 3).reshape(128, 8, 256)),
    }
    return m


_CACHE = {}


def kernel(**inputs):
    NT = SEQ // T
    if NT not in _CACHE:
        _CACHE[NT] = build(NT)
    nc, _ = _CACHE[NT]
    sh = shared_inputs(inputs)
    xfull = np.asarray(inputs["x"], np.float32)
    in_maps = []
    for b in range(NCORES):
        m = dict(sh)
        m["x"] = np.ascontiguousarray(xfull[b])
        in_maps.append(m)
    res = run_bass_kernel_spmd(nc, in_maps, core_ids=list(range(NCORES)))
    return np.stack([np.asarray(r["out"], np.float32) for r in res.results], axis=0)
```

```python
import os
import numpy as np
from contextlib import ExitStack
import concourse.bass as bass
import concourse.mybir as mybir
from concourse.bass_utils import run_bass_kernel_spmd

F32 = mybir.dt.float32
BF16 = mybir.dt.bfloat16
AF = mybir.ActivationFunctionType
ALU = mybir.AluOpType

D = 2048
DFF = 5632
T = 512
KC = D // 128
FC = DFF // 128
INC = 8480
SEQ = 4096
NCORES = 8
SBUF_BASE = 16640
NSLAB = 304
SBUF_BYTES = 229376 - 64

V_L1PRE, V_L1POST, V_LMPRE, V_LMPOST, V_L2PRE, V_L2POST = 0, 16, 32, 48, 64, 80
V_MU, V_W0, V_A0, V_KK, V_KA, V_RK, V_GNW, V_GNB, V_PS = 96, 123, 131, 139, 147, 155, 163, 171, 179
NV = 187


class Op:
    __slots__ = ("eng", "fn", "chan", "deps", "sig", "cnt", "waits", "idx")

    def __init__(self, eng, fn, chan):
        self.eng, self.fn, self.chan = eng, fn, chan
        self.deps = set()
        self.sig = False
        self.cnt = 0
        self.waits = []
        self.idx = -1


def _esize(dt):
    return 2 if dt == BF16 else 4


def ap_keys(ap):
    t = ap.tensor
    tn = type(t).__name__
    pairs = ap.ap
    pst = pairs[0][0]
    npart = pairs[0][1]
    off = ap.offset
    if pst > 0:
        p0 = off // pst
        fo = off % pst
    else:
        p0, fo = 0, off
    halves = []
    if p0 < 64:
        halves.append(0)
    if p0 + npart > 64:
        halves.append(1)
    if tn.startswith("PSum"):
        return [("P", t.name, h) for h in halves]
    lo0 = t.manual_sbuf_range[0]
    es = _esize(ap.dtype)
    ext = 1
    for s, c in pairs[1:]:
        ext += (c - 1) * abs(s)
    lo = lo0 + fo * es
    hi = lo0 + (fo + ext) * es
    ks = []
    for g in range(lo // 256, (hi + 255) // 256):
        for h in halves:
            ks.append((g, h))
    return ks


class Sched:
    ENGS = ("pe", "act", "dve", "pool", "sp")

    def __init__(self):
        self.ops = []
        self.last_w = {}
        self.readers = {}

    def src_of(self, op):
        return op.chan if op.chan is not None else op.eng

    def op(self, eng, fn, r=(), w=(), rk=(), wk=(), chan=None):
        o = Op(eng, fn, chan)
        o.idx = len(self.ops)
        src = self.src_of(o)
        same = (chan is None)
        rkeys = list(rk)
        for a in r:
            rkeys.extend(ap_keys(a))
        wkeys = list(wk)
        for a in w:
            wkeys.extend(ap_keys(a))
        deps = o.deps
        ops = self.ops
        lw = self.last_w
        rd = self.readers
        for k in rkeys:
            x = lw.get(k)
            if x is not None:
                deps.add(x)
        for k in wkeys:
            x = lw.get(k)
            if x is not None:
                a = ops[x]
                if not (same and a.chan is None and a.eng == eng):
                    deps.add(x)
            rs = rd.get(k)
            if rs:
                for s, x in rs.items():
                    if same and s == eng:
                        continue
                    deps.add(x)
        for k in rkeys:
            d = rd.get(k)
            if d is None:
                rd[k] = {src: o.idx}
            else:
                d[src] = o.idx
        for k in wkeys:
            lw[k] = o.idx
            rd[k] = {}
        deps.discard(o.idx)
        ops.append(o)
        return o

    def finalize(self):
        ops = self.ops

        def skip(a, b):
            return a.eng == "pe" and b.eng == "pe" and a.chan is None and b.chan is None

        ordn = {}
        for o in ops:
            s = self.src_of(o)
            ordn[s] = ordn.get(s, 0) + 1
            o.cnt = ordn[s]
        known = {e: {} for e in self.ENGS}
        frozen = {e: {} for e in self.ENGS}
        dirty = {e: False for e in self.ENGS}
        snap = [None] * len(ops)
        needed = set()
        for o in ops:
            kn = known[o.eng]
            need = {}
            for d in o.deps:
                a = ops[d]
                if skip(a, o):
                    continue
                s = self.src_of(a)
                if kn.get(s, 0) >= a.cnt:
                    continue
                if need.get(s, (0, None))[0] < a.cnt:
                    need[s] = (a.cnt, a)
            for s, (c, a) in need.items():
                if kn.get(s, 0) >= c:
                    continue
                o.waits.append(a.idx)
                needed.add(a.idx)
                kn[s] = c
                dirty[o.eng] = True
                base, s_own, c_own = snap[a.idx]
                for s2, c2 in base.items():
                    if kn.get(s2, 0) < c2:
                        kn[s2] = c2
                if s_own is not None and kn.get(s_own, 0) < c_own:
                    kn[s_own] = c_own
            if dirty[o.eng]:
                frozen[o.eng] = dict(kn)
                dirty[o.eng] = False
            snap[o.idx] = (frozen[o.eng], (o.eng if o.chan is None else None), o.cnt)
            o.deps = None
        counts = {}
        for o in ops:
            o.sig = (o.chan is not None) or (o.idx in needed)
            if o.sig:
                s = self.src_of(o)
                counts[s] = counts.get(s, 0) + 1
                o.cnt = counts[s]
        for o in ops:
            o.waits = [(self.src_of(ops[i]), ops[i].cnt) for i in o.waits]
        self.sources = list(counts.keys())
        return counts

    def emit(self, block, sems, final_waits=()):
        per = {e: [] for e in self.ENGS}
        last_cnt = {}
        for o in self.ops:
            per[o.eng].append(o)
            if o.sig:
                last_cnt[self.src_of(o)] = o.cnt
        engs = self.ENGS

        def unit(s):
            return 1 if s in engs else 16

        def run(name, e):
            for o in per[name]:
                for (s, c) in o.waits:
                    e.wait_ge(sems[s], c * unit(s))
                ins = o.fn(e)
                if o.sig:
                    s = self.src_of(o)
                    ins.then_inc(sems[s], unit(s))
            for (en, s) in final_waits:
                if en == name and s in last_cnt:
                    e.wait_ge(sems[s], last_cnt[s] * unit(s))

        block.tensor(lambda e: run("pe", e))
        block.scalar(lambda e: run("act", e))
        block.vector(lambda e: run("dve", e))
        block.gpsimd(lambda e: run("pool", e))
        block.sync(lambda e: run("sp", e))


WSPEC = [("g1", D, DFF), ("u1", D, DFF), ("d1", DFF, D), ("win", D, INC), ("pb", 1024, D),
         ("pa", 1024, D), ("wo", D, D), ("g2", D, DFF), ("u2", D, DFF), ("d2", DFF, D)]


KROWS = {n: K for n, K, M in WSPEC}


def build(NT, stage=3, conv=True, skip=()):
    nc = bass.Bass("TRN2", target_bir_lowering=False)
    S = Sched()
    NTOK = NT * T
    x = nc.dram_tensor("x", [NTOK, D], F32, kind="ExternalInput").ap()
    out = nc.dram_tensor("out", [NTOK, D], F32, kind="ExternalOutput").ap()
    wf, wb = {}, {}
    for n, K, M in WSPEC:
        wf[n] = nc.dram_tensor(n, [K, M], F32, kind="ExternalInput").ap()
    wscs = [nc.dram_tensor("wsc%d" % i, [76, 128, 4096], BF16, kind="Internal").ap() for i in range(4)]
    vec_d = nc.dram_tensor("vec", [128, NV], F32, kind="ExternalInput").ap()
    wa2_d = nc.dram_tensor("wa2", [128, 1024], F32, kind="ExternalInput").ap()
    g2a_d = nc.dram_tensor("g2a", [128, 1024], F32, kind="ExternalInput").ap()
    g2b_d = nc.dram_tensor("g2b", [32, 1024], F32, kind="ExternalInput").ap()
    pw_d = nc.dram_tensor("poolw", [128, 8, 256], F32, kind="ExternalInput").ap()

    cur = [SBUF_BASE]
    cnt = [0]

    def alloc(shape, dt, at=None):
        n = 1
        for s in shape[1:]:
            n *= s
        nb = n * _esize(dt)
        nb = (nb + 63) // 64 * 64
        if at is None:
            if nb >= 1024:
                cur[0] = (cur[0] + 255) // 256 * 256
                nb = (nb + 255) // 256 * 256
            at = cur[0]
            cur[0] += nb
            assert cur[0] <= SBUF_BYTES, ("SBUF overflow", cur[0])
        cnt[0] += 1
        return nc.alloc_sbuf_tensor_at("t%d" % cnt[0], list(shape), dt, offset=at)

    vec = alloc([128, NV], F32)
    cst = alloc([128, 8], F32)
    identf = alloc([128, 128], F32)
    identb = alloc([128, 128], BF16)
    onesb = alloc([128, 128], BF16)
    blkb = alloc([128, 128], BF16)
    mask12 = alloc([128, 8, 128], BF16)
    mask3 = alloc([128, 8, 64], BF16)
    irep = alloc([128, 8, 64], BF16)
    rmask = alloc([128, T], BF16)
    wa2 = alloc([128, 1024], BF16)
    g2a = alloc([128, 1024], BF16)
    g2b = alloc([128, 1024], BF16)
    poolw = alloc([128, 8, 256], BF16)
    zc = alloc([128, 32], F32)
    hist = alloc([128, 8, 16], F32)
    H0 = alloc([128, 8, 64], F32)
    WCt = alloc([128, 8, 8], F32)
    rs = alloc([128, T], F32)
    sqs = [alloc([128, T], BF16) for _ in range(2)]
    h = alloc([128, KC, T], F32)
    xn = alloc([128, KC, T], BF16)
    stg = [alloc([128, 1024], F32) for _ in range(2)]
    NSLOT = 3
    wslots = [alloc([128, 4096], BF16) for _ in range(NSLOT)]
    base = cur[0]
    act_t = alloc([128, FC, T], BF16)
    f_t = alloc([128, KC, T], F32)
    ffn_end = cur[0]
    sgs = [alloc([128, T], F32) for _ in range(2)]
    cur[0] = base
    ARt = alloc([128, 8, 8, 128], BF16)
    Bt = alloc([128, 8, T], BF16)
    Kt = alloc([128, 8, T], BF16)
    Vt = alloc([128, 8, T], BF16)
    BON = alloc([128, 8, T], BF16)
    GG = alloc([128, 8, T], BF16)
    Y = alloc([128, 8, T], F32)
    mix_end = cur[0]
    MIX = alloc([128, 8, T], BF16, at=Kt.manual_sbuf_range[0])
    YB = alloc([128, 8, T], BF16, at=Vt.manual_sbuf_range[0])
    Mg = alloc([128, KC, T], BF16, at=ARt.manual_sbuf_range[0])
    YA = alloc([128, 8, T], BF16, at=Bt.manual_sbuf_range[0])
    MX = alloc([128, KC, T], F32, at=Kt.manual_sbuf_range[0])
    assert Kt.manual_sbuf_range[0] + 32768 <= Y.manual_sbuf_range[0]
    tmp0 = cur[0]
    nm = {k: alloc([128, 528 if k in ("R", "K", "V") else T], F32)
          for k in ("R", "K", "V", "A", "LW", "KKN", "KN", "CUM")}
    nm["E"] = nm["CUM"]
    cumc = alloc([128, 8], F32)
    scr = [alloc([128, 528], F32) for _ in range(4)]
    LAT = alloc([128, T], BF16)
    SGA = alloc([128, T], BF16)
    SGB = alloc([128, T], BF16)
    scb = sqs
    prep_end = cur[0]
    cur[0] = tmp0
    SC1 = alloc([128, 8, 128], BF16)
    SC2 = alloc([128, 8, 128], BF16)
    Pb = [alloc([128, 8, 64], BF16) for _ in range(2)]
    Qb = [alloc([128, 8, 64], BF16) for _ in range(2)]
    Gb = [alloc([128, 8, 64], BF16) for _ in range(2)]
    VT = alloc([128, 8, 64], BF16)
    BT = alloc([128, 8, 64], BF16)
    KT = alloc([128, 8, 64], BF16)
    X1b = alloc([128, 8, 64], BF16)
    Ub = alloc([128, 8, 64], BF16)
    H0p = alloc([128, 8, 64], F32)
    H0pb = alloc([128, 8, 64], BF16)
    cur[0] = max(prep_end, cur[0], ffn_end + 2 * 2048)
    xs0 = (ffn_end + 2 * 2048 + 255) // 256 * 256
    stg_x = []
    while xs0 + 4096 <= prep_end and len(stg_x) < 5:
        cnt[0] += 1
        stg_x.append(nc.alloc_sbuf_tensor_at("t%d" % cnt[0], [128, 1024], F32, offset=xs0))
        xs0 += 4096
    stg_all = stg + stg_x
    in_ffn = [False]
    assert cur[0] <= SBUF_BYTES, cur[0]

    psum = [nc.alloc_psum_tensor("ps%d" % i, [128, 512], F32) for i in range(8)]
    pctr = [0]

    def PS():
        p = psum[pctr[0] % 8]
        pctr[0] += 1
        return p

    rot = {"scr": 0, "scb": 0, "sq": 0, "sg": 0, "ws": 0, "stg": 0}

    def nxt(name, lst):
        v = lst[rot[name] % len(lst)]
        rot[name] += 1
        return v

    def mm(o, lhsT, rhs, start=True, stop=True):
        S.op("pe", lambda e: e.matmul(o, lhsT=lhsT, rhs=rhs, start=start, stop=stop), r=[lhsT, rhs], w=[o])

    def tr(o, in_, ident):
        S.op("pe", lambda e: e.transpose(o, in_, ident), r=[in_, ident], w=[o])

    def act(o, in_, func, bias=None, scale=None, eng="act"):
        r = [in_]
        kw = {}
        if bias is not None:
            kw["bias"] = bias
            r.append(bias)
        if scale is not None:
            kw["scale"] = scale
            if not isinstance(scale, float):
                r.append(scale)
        S.op("act", lambda e: e.activation(out=o, in_=in_, func=func, **kw), r=r, w=[o])

    def cp(eng, o, in_):
        if eng == "act":
            S.op("act", lambda e: e.copy(out=o, in_=in_), r=[in_], w=[o])
        else:
            S.op(eng, lambda e: e.tensor_copy(out=o, in_=in_), r=[in_], w=[o])

    def tt(eng, o, a, b, op):
        S.op(eng, lambda e: e.tensor_tensor(out=o, in0=a, in1=b, op=op), r=[a, b], w=[o])

    def ts(eng, o, a, s1, s2, op0, op1=None):
        r = [a] + [s for s in (s1, s2) if s is not None and not isinstance(s, float)]
        if op1 is None:
            S.op(eng, lambda e: e.tensor_scalar(out=o, in0=a, scalar1=s1, scalar2=None, op0=op0), r=r, w=[o])
        else:
            S.op(eng, lambda e: e.tensor_scalar(out=o, in0=a, scalar1=s1, scalar2=s2, op0=op0, op1=op1), r=r, w=[o])

    def stt(o, in0, sc, in1, op0, op1):
        r = [in0, in1] + ([] if isinstance(sc, float) else [sc])
        S.op("dve", lambda e: e.scalar_tensor_tensor(out=o, in0=in0, scalar=sc, in1=in1, op0=op0, op1=op1), r=r, w=[o])

    def recip(o, in_):
        S.op("dve", lambda e: e.reciprocal(out=o, in_=in_), r=[in_], w=[o])

    def memset(eng, o, v):
        S.op(eng, lambda e: e.memset(o, v), w=[o])

    def vcol(c):
        return vec[:, c:c + 1]

    S.op("sp", lambda e: e.dma_start(out=vec[:], in_=vec_d), w=[vec[:]], chan="c0")
    memset("pool", cst[:, 0:1], 1e-6)
    memset("pool", cst[:, 1:2], 64e-5)
    memset("pool", cst[:, 2:3], 0.0)
    memset("pool", identf[:], 1.0)
    S.op("pool", lambda e: e.affine_select(out=identf[:], in_=identf[:], pattern=[[-1, 128]], compare_op=ALU.is_equal,
                                           fill=0.0, base=0, channel_multiplier=1), r=[identf[:]], w=[identf[:]])
    memset("pool", onesb[:], 1.0)
    memset("pool", blkb[:], 0.0)
    memset("pool", blkb[0:64, 0:64], 1.0)
    memset("pool", blkb[64:128, 64:128], 1.0)
    memset("pool", rmask[:], 1.0)
    memset("pool", rmask[:].rearrange("p (c t) -> p c t", t=64)[:, :, 0:1], 0.0)
    memset("pool", zc[:], 0.0)
    memset("pool", hist[:], 0.0)
    memset("pool", H0[:], 0.0)
    mtmp = [nc.alloc_sbuf_tensor_at("mtmp%d" % i, [128, 8, 64], F32, offset=base + i * 2048) for i in range(9)]
    onesf = mtmp[8][:]
    memset("pool", onesf, 1.0)
    mspecs = [(mask12[:, :, 0:64], ALU.is_gt, -1, 1),
              (mask12[:, :, 64:128], ALU.is_ge, -1, 1),
              (mask3[:], ALU.is_gt, 1, -1),
              (irep[:], ALU.is_equal, 1, -1)]

    def asel(o, cmp_, cm, st, base_, last=False):
        S.op("pool", lambda e: e.affine_select(out=o, in_=onesf, pattern=[[0, 8], [st, 64]], compare_op=cmp_,
                                               fill=0.0, base=base_, channel_multiplier=cm), r=[onesf], w=[o],
             wk=(["POOLDONE"] if last else []))

    for i, (dst, cmp_, cm, st) in enumerate(mspecs):
        asel(mtmp[2 * i][:], cmp_, cm, st, 0)
        asel(mtmp[2 * i + 1][:], cmp_, cm, st, -64 * cm, last=(i == 3))
    S.op("dve", lambda e: e.tensor_copy(out=identb[:], in_=identf[:]), r=[identf[:]], w=[identb[:]], rk=["POOLDONE"])
    for i, (dst, cmp_, cm, st) in enumerate(mspecs):
        cp("dve", dst[0:64], mtmp[2 * i][0:64])
        cp("dve", dst[64:128], mtmp[2 * i + 1][64:128])

    first_tile = [True]
    castrr = [0]

    def stage_cast(dst_view, src_ap, np_=128):
        nst = len(stg_all) if in_ffn[0] else 2
        si = rot["stg"] % nst
        rot["stg"] += 1
        st_ = stg_all[si]
        shp = list(dst_view.shape)
        nel = 1
        for v_ in shp[1:]:
            nel *= v_
        sv = st_[0:np_, 0:nel]
        if len(shp) == 3:
            sv = sv.rearrange("p (c m) -> p c m", m=shp[2])
        S.op("sp", lambda e: e.dma_start(out=sv, in_=src_ap), w=[sv], chan="sg%d" % si)
        castrr[0] += 1
        cp("act" if castrr[0] % 2 else "dve", dst_view, sv)

    slab_ids = {}
    pend_store = [None]

    def load_slab(n, k0, nkc, m0, mw):
        slot = nxt("ws", list(range(NSLOT)))
        nel = nkc * mw
        flat = wslots[slot][:, 0:nel]
        view = flat.rearrange("p (c m) -> p c m", m=mw)
        key = ("wb", n, k0, m0)
        if key not in slab_ids:
            slab_ids[key] = len(slab_ids)
            assert len(slab_ids) <= NSLAB
        sid = slab_ids[key]
        dram_b = wscs[sid // 76][sid % 76, :, 0:nel]
        if first_tile[0]:
            npc = 1024 // mw
            for c0 in range(0, nkc, npc):
                n_ = min(npc, nkc - c0)
                src = wf[n][(k0 + c0) * 128:(k0 + c0 + n_) * 128, m0:m0 + mw].rearrange("(c p) m -> p c m", p=128)
                stage_cast(view[:, c0:c0 + n_, :], src)
            if pend_store[0] is not None:
                pend_store[0]()
            pend_store[0] = lambda: S.op("sp", lambda e: e.dma_start(out=dram_b, in_=flat), r=[flat], wk=[key],
                                         chan="wst%d" % slot)
        else:
            S.op("sp", lambda e: e.dma_start(out=flat, in_=dram_b), rk=[key], w=[flat], chan="ws%d" % slot)
        return view

    stage_cast(wa2[:], wa2_d)
    stage_cast(g2a[:], g2a_d)
    stage_cast(g2b[0:32, :], g2b_d, np_=32)
    stage_cast(poolw[:, 0:4, :], pw_d[:, 0:4, :])
    stage_cast(poolw[:, 4:8, :], pw_d[:, 4:8, :])

    def linear(n, rhs_chunks, m0, mtot, consume, mw=256):
        nk = len(rhs_chunks)
        parts = [(k0, min(16, nk - k0)) for k0 in range(0, nk, 16)]
        for s0 in range(0, mtot, mw):
            w_ = min(mw, mtot - s0)
            nj = (w_ + 127) // 128
            pss = [PS() for _ in range(nj)]
            for (k0, nkc) in parts:
                slab = load_slab(n, k0, nkc, m0 + s0, w_)
                for j in range(nj):
                    cw = min(128, w_ - j * 128)
                    for kc in range(nkc):
                        mm(pss[j][0:cw, :], slab[:, kc, j * 128:j * 128 + cw], rhs_chunks[k0 + kc],
                           start=(k0 + kc == 0), stop=(k0 + kc == nk - 1))
            for j in range(nj):
                consume(s0 // 128 + j, pss[j])

    def stats(chunks, n_feat):
        p = PS()
        for c, ch in enumerate(chunks):
            sq = nxt("sq", sqs)
            if c % 2 == 0:
                act(sq[:], ch, AF.Square)
            else:
                tt("dve", sq[:], ch, ch, ALU.mult)
            mm(p[:], onesb[:], sq[:], start=(c == 0), stop=(c == len(chunks) - 1))
        act(rs[:], p[:], AF.Sqrt, bias=cst[:, 0:1], scale=1.0 / n_feat)
        recip(rs[:], rs[:])

    def prenorm(gcol):
        stats([h[:, c, :] for c in range(KC)], D)
        for c in range(KC):
            stt(xn[:, c, :], h[:, c, :], vcol(gcol + c), rs[:], ALU.mult, ALU.mult)

    def postnorm_residual(src, gcol, wgt):
        stats([src[:, c, :] for c in range(KC)], D)
        for c in range(KC):
            t1 = nxt("scr", scr)[:, 0:T]
            stt(t1, src[:, c, :], vcol(gcol + c), rs[:], ALU.mult, ALU.mult)
            stt(h[:, c, :], t1, wgt, h[:, c, :], ALU.mult, ALU.add)

    def ffn(gn, un, dn, pre, post):
        in_ffn[0] = True
        prenorm(pre)
        xch = [xn[:, c, :] for c in range(KC)]
        pend = {}

        def cons_g(j, p):
            pend[j] = p

        for jb in range(FC // 2):
            def cons_u(j, p, jb=jb):
                sg = nxt("sg", sgs)
                act(sg[:], pend.pop(j)[:], AF.Silu)
                tt("dve", act_t[:, jb * 2 + j, :], sg[:], p[:], ALU.mult)

            linear(gn, xch, jb * 256, 256, cons_g)
            linear(un, xch, jb * 256, 256, cons_u)

        def cons_d(j, p):
            cp("act", f_t[:, j, :], p[:])

        linear(dn, [act_t[:, c, :] for c in range(FC)], 0, D, cons_d)
        in_ffn[0] = False
        postnorm_residual(f_t, post, 0.5)

    def lerp(p, np_, zidx, o):
        z1 = nxt("scr", scr)
        cp("dve", z1[0:np_, 0:1], zc[0:np_, zidx:zidx + 1])
        cp("act", z1[0:np_, 1:T + 1], p[0:np_, :])
        cp("dve", zc[0:np_, zidx:zidx + 1], z1[0:np_, T:T + 1])
        d = nxt("scr", scr)
        tt("dve", d[0:np_, 0:T], z1[0:np_, 0:T], p[0:np_, :], ALU.subtract)
        stt(o, d[0:np_, 0:T], vec[0:np_, V_MU + zidx:V_MU + zidx + 1], z1[0:np_, 1:T + 1], ALU.mult, ALU.add)

    def c3(ap_):
        return ap_.rearrange("p (c t) -> p c t", t=64)

    def mixer(ti):
        MST = int(os.environ.get('MST', '9'))
        prenorm(V_LMPRE)
        un_ = [xn[:, c, :] for c in range(KC)]
        zl = nm["E"]

        def cons_lat(j, p):
            lerp(p, 128, 24, zl[:])
            act(LAT[0:64, :], zl[0:64, :], AF.Tanh)
            cp("act", LAT[64:128, :], zl[64:128, :])

        linear("win", un_, 3072, 128, cons_lat)

        def cons_gl(j, p):
            if j == 0:
                lerp(p, 128, 25, zl[:])
                act(SGA[:], zl[:], AF.Sigmoid)
            else:
                lerp(p, 32, 26, zl[0:32, :])
                act(SGB[0:32, :], zl[0:32, :], AF.Sigmoid)

        linear("win", un_, 3200, 160, cons_gl)

        if MST < 3:
            return
        R, K_, V_, A_, LW, KKN, KN, CUM, E_ = (nm[k] for k in ("R", "K", "V", "A", "LW", "KKN", "KN", "CUM", "E"))
        R, K_, V_ = R[:, 0:T], K_[:, 0:T], V_[:, 0:T]
        for fc in range(8):
            got = {}

            def mk(name):
                def c_(j, p):
                    got[(name, j)] = p
                return c_

            linear("win", un_, fc * 128, 128, mk("r"), mw=128)
            linear("win", un_, 1024 + fc * 128, 128, mk("k"), mw=128)
            linear("win", un_, 2048 + fc * 128, 128, mk("v"), mw=128)
            for j in range(1):
                cols = slice(fc * 128, (fc + 1) * 128)
                lerp(got[("r", j)], 128, fc, R)
                lerp(got[("k", j)], 128, 8 + fc, K_)
                lerp(got[("v", j)], 128, 16 + fc, V_)
                pw = PS()
                mm(pw[:], wa2[0:64, cols], LAT[0:64, :])
                pa_ = PS()
                mm(pa_[:], wa2[64:128, cols], LAT[64:128, :])
                pg = PS()
                mm(pg[:], g2a[:, cols], SGA[:], start=True, stop=False)
                mm(pg[:], g2b[0:32, cols], SGB[0:32, :], start=False, stop=True)
                act(LW[:], pw[:], AF.Sigmoid, bias=vcol(V_W0 + fc))
                act(A_[:], pa_[:], AF.Sigmoid, bias=vcol(V_A0 + fc))
                cp("act", GG[:, fc, :], pg[:])
                ts("dve", LW[:], LW[:], -0.6065306597126334, None, ALU.mult)
                ts("dve", KKN[:], K_, vcol(V_KK + fc), None, ALU.mult)
                sqb = nxt("scb", scb)
                act(sqb[:], KKN[:], AF.Square)
                pn = PS()
                mm(pn[:], blkb[:], sqb[:])
                t2 = nxt("scr", scr)[:, 0:T]
                act(t2, pn[:], AF.Sqrt, bias=cst[:, 2:3], scale=1.0)
                ts("dve", t2, t2, 1e-12, None, ALU.max)
                recip(t2, t2)
                tt("dve", KKN[:], KKN[:], t2, ALU.mult)
                t3 = nxt("scr", scr)[:, 0:T]
                ts("dve", t3, A_[:], 1.0, vcol(V_KA + fc), ALU.subtract, ALU.mult)
                stt(KN[:], t3, 1.0, K_, ALU.add, ALU.mult)
                rkb = nxt("scb", scb)
                stt(rkb[:], R, vcol(V_RK + fc), KN[:], ALU.mult, ALU.mult)
                pb_ = PS()
                mm(pb_[:], blkb[:], rkb[:])
                tt("dve", BON[:, fc, :], pb_[:], V_, ALU.mult)
                S.op("dve", lambda e: e.tensor_tensor_scan(out=CUM[:], data0=rmask[:], data1=LW[:], initial=0.0,
                                                           op0=ALU.mult, op1=ALU.add), r=[rmask[:], LW[:]], w=[CUM[:]])
                cp("dve", cumc[:], c3(CUM[:])[:, :, 63])
                act(WCt[:, fc, :], cumc[:], AF.Exp)
                tt("dve", c3(E_[:]), cumc[:].unsqueeze(2).to_broadcast([128, 8, 64]), c3(CUM[:]), ALU.subtract)
                t5 = nxt("scr", scr)[:, 0:T]
                act(t5, E_[:], AF.Exp)
                tt("dve", Kt[:, fc, :], KN[:], t5, ALU.mult)
                t6 = nxt("scr", scr)[:, 0:T]
                tt("dve", t6, KKN[:], A_[:], ALU.mult)
                tt("dve", Bt[:, fc, :], t6, t5, ALU.mult)
                t7 = nxt("scr", scr)[:, 0:T]
                act(t7, E_[:], AF.Exp, scale=-1.0)
                tt("dve", ARt[:, fc, :, 64:128], c3(R), c3(t7), ALU.mult)
                t8 = nxt("scr", scr)[:, 0:T]
                tt("dve", t8, E_[:], LW[:], ALU.add)
                act(t8, t8, AF.Exp, scale=-1.0)
                stt(ARt[:, fc, :, 0:64], c3(KKN[:]), -1.0, c3(t8), ALU.mult, ALU.mult)
                cp("act", Vt[:, fc, :], V_)

        if MST < 4:
            return
        for c in range(T // 64):
            cs = slice(c * 64, (c + 1) * 64)
            s1 = [PS(), PS()]
            s2 = [PS(), PS()]
            s3 = PS()
            for hp in range(8):
                for par in range(2):
                    P_ = slice(par * 64, par * 64 + 64)
                    o1 = s1[hp // 4][P_, (hp % 4) * 128:(hp % 4 + 1) * 128]
                    o2 = s2[hp // 4][P_, (hp % 4) * 128:(hp % 4 + 1) * 128]
                    mm(o1, Bt[P_, hp, cs], ARt[P_, hp, c, :])
                    mm(o2, Kt[P_, hp, cs], ARt[P_, hp, c, :])
                    mm(s3[P_, hp * 64:(hp + 1) * 64], ARt[P_, hp, c, 0:64], Bt[P_, hp, cs])
            for q in range(2):
                hs = slice(q * 4, q * 4 + 4)
                tt("dve", SC1[:, hs, :], s1[q][:].rearrange("p (a b) -> p a b", b=128), mask12[:, hs, :], ALU.mult)
                tt("dve", SC2[:, hs, :], s2[q][:].rearrange("p (a b) -> p a b", b=128), mask12[:, hs, :], ALU.mult)
            P0, Q0, G0 = Pb[0], SC1, Gb[0]
            tt("dve", P0[:], s3[:].rearrange("p (a b) -> p a b", b=64), mask3[:], ALU.mult)
            tt("dve", G0[:], SC1[:, :, 0:64], irep[:], ALU.add)
            for (src, dst) in ((Vt, VT), (Bt, BT), (Kt, KT)):
                p = PS()
                for hp in range(8):
                    for par in range(2):
                        P_ = slice(par * 64, par * 64 + 64)
                        mm(p[P_, hp * 64:(hp + 1) * 64], src[P_, hp, cs], identb[P_, par * 64:par * 64 + 64])
                cp("act", dst[:], p[:].rearrange("p (a b) -> p a b", b=64))

            def batch(lhs_fn, rhs_fn):
                p = PS()
                for hp in range(8):
                    for par in range(2):
                        P_ = slice(par * 64, par * 64 + 64)
                        mm(p[P_, hp * 64:(hp + 1) * 64], lhs_fn(P_, hp), rhs_fn(P_, hp))
                return p[:].rearrange("p (a b) -> p a b", b=64)

            def vw(t_):
                return lambda P_, hp: t_[P_, hp, :]

            Qc = lambda P_, hp: SC1[P_, hp, 0:64]
            Pc = vw(Pb[0])
            Gc = Gb[0]
            for lvl in range(1, 6):
                pp = batch(Qc, Pc)
                pq = batch(Pc, Qc) if lvl < 5 else None
                Pn = Pb[lvl % 2]
                cp("act", Pn[:], pp)
                if pq is not None:
                    Qn = Qb[lvl % 2]
                    cp("act", Qn[:], pq)
                    Qc = vw(Qn)
                Pc = vw(Pn)
                pgm = batch(Pc, vw(Gc))
                Gn = Gb[lvl % 2]
                tt("dve", Gn[:], pgm, Gc[:], ALU.add)
                Gc = Gn
            tt("dve", H0p[:], H0[:], WCt[:, :, c:c + 1].to_broadcast([128, 8, 64]), ALU.mult)
            cp("act", H0pb[:], H0p[:])
            p = PS()
            for hp in range(8):
                for par in range(2):
                    P_ = slice(par * 64, par * 64 + 64)
                    o = p[P_, hp * 64:(hp + 1) * 64]
                    mm(o, ARt[P_, hp, c, 0:64], H0pb[P_, hp, :], start=True, stop=False)
                    mm(o, SC2[P_, hp, 0:64], VT[P_, hp, :], start=False, stop=True)
            cp("act", X1b[:], p[:].rearrange("p (a b) -> p a b", b=64))
            pu = batch(lambda P_, hp: Gc[P_, hp, :], lambda P_, hp: X1b[P_, hp, :])
            cp("act", Ub[:], pu)
            py = PS()
            ph = PS()
            for hp in range(8):
                for par in range(2):
                    P_ = slice(par * 64, par * 64 + 64)
                    o = py[P_, hp * 64:(hp + 1) * 64]
                    mm(o, H0pb[P_, hp, :], ARt[P_, hp, c, 64:128], start=True, stop=False)
                    mm(o, Ub[P_, hp, :], SC1[P_, hp, 64:128], start=False, stop=False)
                    mm(o, VT[P_, hp, :], SC2[P_, hp, 64:128], start=False, stop=True)
                    o2 = ph[P_, hp * 64:(hp + 1) * 64]
                    mm(o2, BT[P_, hp, :], Ub[P_, hp, :], start=True, stop=False)
                    mm(o2, KT[P_, hp, :], VT[P_, hp, :], start=False, stop=True)
            cp("act", Y[:, :, cs], py[:].rearrange("p (a b) -> p a b", b=64))
            tt("dve", H0[:], ph[:].rearrange("p (a b) -> p a b", b=64), H0p[:], ALU.add)

        if MST < 5:
            return
        P5N = int(os.environ.get('P5N', '99'))
        for fc in range(8):
            yb_ = nxt("scb", scb)
            ysq = nxt("scb", scb)
            pm = PS()
            pq_ = PS()
            d_ = nxt("scr", scr)[:, 0:T]
            msq = nxt("scr", scr)[:, 0:T]
            var = nxt("scr", scr)[:, 0:T]
            steps = [
                lambda: cp("act", yb_[:], Y[:, fc, :]),
                lambda: mm(pm[:], blkb[:], yb_[:]),
                lambda: act(ysq[:], Y[:, fc, :], AF.Square),
                lambda: mm(pq_[:], blkb[:], ysq[:]),
                lambda: ts("dve", msq, pm[:], 1.0 / 64, None, ALU.mult),
                lambda: tt("dve", d_, Y[:, fc, :], msq, ALU.subtract),
                lambda: act(msq, msq, AF.Square),
                lambda: stt(var, pq_[:], 1.0 / 64, msq, ALU.mult, ALU.subtract),
                lambda: act(var, var, AF.Sqrt, bias=cst[:, 1:2], scale=1.0),
                lambda: recip(var, var),
                lambda: tt("dve", d_, d_, var, ALU.mult),
                lambda: act(d_, d_, AF.Identity, bias=vcol(V_GNB + fc), scale=vcol(V_GNW + fc)),
                lambda: tt("dve", d_, d_, BON[:, fc, :], ALU.add),
                lambda: tt("dve", YA[:, fc, :], d_, GG[:, fc, :], ALU.mult),
            ]
            for st_ in steps[:P5N]:
                st_()

        if MST < 2:
            return
        def cons_pool(j, p, gi):
            c = gi * 2 + j
            wdw = (2, 4, 8, 16)[gi]
            zb = nm["R"]
            sab = (nm["K"], nm["V"])
            cp("dve", zb[:, 1:16], hist[:, c, 1:16])
            cp("act", zb[:, 16:528], p[:])
            cp("dve", hist[:, c, 1:16], zb[:, 513:528])
            prev = zb
            sh = 1
            ia = 0
            while sh < wdw:
                nx_ = sab[ia % 2]
                ia += 1
                lo = 2 * sh
                tt("dve", nx_[:, lo:528], prev[:, lo:528], prev[:, lo - sh:528 - sh], ALU.add)
                prev = nx_
                sh *= 2
            stt(MIX[:, c, :], prev[:, 16:528], 1.0 / wdw, zb[:, 16:528], ALU.mult, ALU.subtract)
            if ti == 0:
                for t_ in range(wdw - 1):
                    stt(MIX[:, c, t_:t_ + 1], prev[:, 16 + t_:17 + t_], 1.0 / (t_ + 1), zb[:, 16 + t_:17 + t_],
                        ALU.mult, ALU.subtract)

        for gi in range(4):
            linear("win", un_, 3360 + gi * 256, 256, (lambda g_: (lambda j, p: cons_pool(j, p, g_)))(gi))
        for gi in range(4):
            for mo in range(2):
                p = PS()
                for ki in range(2):
                    mm(p[:], poolw[:, gi * 2 + ki, mo * 128:(mo + 1) * 128], MIX[:, gi * 2 + ki, :],
                       start=(ki == 0), stop=(ki == 1))
                ts("dve", YB[:, gi * 2 + mo, :], p[:], vcol(V_PS + gi * 2 + mo), None, ALU.mult)

        if MST < 6:
            return
        ya_ = [YA[:, c, :] for c in range(8)]
        yb_c = [YB[:, c, :] for c in range(8)]
        for ob in range(8):
            got = {}

            def mk2(name):
                def c_(j, p):
                    got[(name, j)] = p
                return c_

            linear("pa", ya_, ob * 256, 256, mk2("a"))
            linear("pb", yb_c, ob * 256, 256, mk2("b"))
            linear("win", un_, 4384 + ob * 256, 256, mk2("ga"))
            linear("win", un_, 6432 + ob * 256, 256, mk2("gb"))
            for j in range(2):
                oc = ob * 2 + j
                sa = nxt("scr", scr)[:, 0:T]
                act(sa, got[("ga", j)][:], AF.Sigmoid)
                sb_ = nxt("scr", scr)[:, 0:T]
                act(sb_, got[("gb", j)][:], AF.Sigmoid)
                tt("dve", sa, sa, got[("a", j)][:], ALU.mult)
                tt("dve", sb_, sb_, got[("b", j)][:], ALU.mult)
                tt("dve", Mg[:, oc, :], sa, sb_, ALU.add)

        def cons_o(j, p):
            cp("act", MX[:, j, :], p[:])

        linear("wo", [Mg[:, c, :] for c in range(KC)], 0, D, cons_o)
        postnorm_residual(MX, V_LMPOST, 1.0)

    xs = f_t[:].rearrange("p c t -> p (c t)").rearrange("p (j d) -> p j d", d=D)
    ot_t = nc.alloc_sbuf_tensor_at("ot", [128, 4, D], F32, offset=act_t.manual_sbuf_range[0])
    def load_x(ti_):
        for j in range(4):
            src = x[ti_ * T + j * 128:ti_ * T + (j + 1) * 128, :]
            S.op("sp", (lambda s_, j_: (lambda e: e.dma_start(out=xs[:, j_, :], in_=s_)))(src, j), w=[xs[:, j, :]],
                 chan="xl%d" % j)

    load_x(0)
    for ti in range(NT):
        t0 = ti * T
        first_tile[0] = (ti == 0)
        for c in range(KC):
            p = PS()
            for j in range(4):
                tr(p[:, j * 128:(j + 1) * 128], xs[:, j, c * 128:(c + 1) * 128], identf[:])
            cp("act" if c % 2 else "dve", h[:, c, :], p[:])
        if stage >= 1:
            ffn("g1", "u1", "d1", V_L1PRE, V_L1POST)
        if stage >= 2:
            mixer(ti)
        if stage >= 3:
            ffn("g2", "u2", "d2", V_L2PRE, V_L2POST)
        if pend_store[0] is not None:
            pend_store[0]()
            pend_store[0] = None
        if ti + 1 < NT:
            load_x(ti + 1)
        for j in range(4):
            for cb in range(4):
                p = PS()
                for c4 in range(4):
                    c = cb * 4 + c4
                    tr(p[:, c4 * 128:(c4 + 1) * 128], h[:, c, j * 128:(j + 1) * 128], identf[:])
                cp("act" if cb % 2 else "dve", ot_t[:, j, cb * 512:(cb + 1) * 512], p[:])
        dst = out[t0:t0 + T, :].rearrange("(j p) d -> p j d", p=128)
        S.op("sp", (lambda d_: (lambda e: e.dma_start(out=d_, in_=ot_t[:])))(dst), r=[ot_t[:]], chan="os")

    S.finalize()
    st = ExitStack()
    sems = {s: st.enter_context(nc.semaphore("s_" + s)) for s in S.sources}
    block = st.enter_context(nc.Block())
    S.emit(block, sems, final_waits=[("sp", "os")])
    st.close()
    return nc, S


def make_vec(inp):
    v = np.zeros((128, NV), np.float32)

    def put(col, a):
        a = np.asarray(a, np.float32).reshape(-1)
        n = (a.size + 127) // 128
        pad = np.zeros(n * 128, np.float32)
        pad[:a.size] = a
        v[:, col:col + n] = pad.reshape(n, 128).T

    put(V_L1PRE, inp["ln_ffn1_pre"][0])
    put(V_L1POST, inp["ln_ffn1_post"][0])
    put(V_LMPRE, inp["ln_mix_pre"][0])
    put(V_LMPOST, inp["ln_mix_post"][0])
    put(V_L2PRE, inp["ln_ffn2_pre"][0])
    put(V_L2POST, inp["ln_ffn2_post"][0])
    put(V_MU, inp["rwkv_mu"][0])
    put(V_W0, inp["rwkv_w0"][0])
    put(V_A0, inp["rwkv_a0"][0])
    put(V_KK, inp["rwkv_k_k"][0])
    put(V_KA, inp["rwkv_k_a"][0])
    put(V_RK, inp["rwkv_r_k"][0])
    put(V_GNW, inp["rwkv_gn_w"][0])
    put(V_GNB, inp["rwkv_gn_b"][0])
    put(V_PS, inp["pool_scale"][0])
    return v


def shared_inputs(inp):
    f = lambda a: np.ascontiguousarray(np.asarray(a, np.float32))
    m = {
        "g1": f(inp["ffn1_gate"][0]), "u1": f(inp["ffn1_up"][0]), "d1": f(inp["ffn1_down"][0]),
        "win": f(inp["w_in"][0]), "pa": f(inp["w_proj_a"][0]), "pb": f(inp["w_proj_b"][0]),
        "wo": f(inp["w_out"][0]),
        "g2": f(inp["ffn2_gate"][0]), "u2": f(inp["ffn2_up"][0]), "d2": f(inp["ffn2_down"][0]),
        "vec": make_vec(inp),
        "wa2": f(np.concatenate([np.asarray(inp["rwkv_w2"][0]), np.asarray(inp["rwkv_a2"][0])], axis=0)),
        "g2a": f(np.asarray(inp["rwkv_g2"][0])[0:128]),
        "g2b": f(np.asarray(inp["rwkv_g2"][0])[128:160]),
        "poolw": f(np.asarray(inp["pool_w"][0]).reshape(4, 2, 128, 256).transpose(2, 0, 1, 3).reshape(128, 8, 256)),
    }
    return m


_CACHE = {}


def kernel(**inputs):
    NT = SEQ // T
    if NT not in _CACHE:
        _CACHE[NT] = build(NT)
    nc, _ = _CACHE[NT]
    sh = shared_inputs(inputs)
    xfull = np.asarray(inputs["x"], np.float32)
    in_maps = []
    for b in range(NCORES):
        m = dict(sh)
        m["x"] = np.ascontiguousarray(xfull[b])
        in_maps.append(m)
    res = run_bass_kernel_spmd(nc, in_maps, core_ids=list(range(NCORES)))
    return np.stack([np.asarray(r["out"], np.float32) for r in res.results], axis=0)
```
